# Optimizing a Trainium2 kernel written in Bass

```python
import math
import jax, jax.numpy as jnp
from jax import lax
import numpy as np

D_MODEL = 2048
BATCH = 2
SEQ = 4096
DEPTH = 2
DEC_BATCH = 4
DEC_SEQ = 2048
PAST_LEN = 128

A_HEADS = 8
A_HEAD_DIM = 128
A_WIDTH = A_HEADS * A_HEAD_DIM
A_CONV = 3
GDN_CHUNK = 64
B_WIDTH = 1024
B_CONV = 3
C_GROUPS = ((128, 1), (512, 4), (2048, 16))
C_HEADS_PER_GROUP = 4
C_HEADS = C_HEADS_PER_GROUP * 3
C_HEAD_DIM = 128
C_WIDTH = C_HEADS * C_HEAD_DIM
C_OUT = C_HEADS_PER_GROUP * C_HEAD_DIM
N_BUCKETS = 32
REL_MAX_DIST = 2048
N_BRANCHES = 3
D_FF = 4 * D_MODEL
EPS = 1e-6
NEG_INF = -1e30
SPLIT_SIZES = (3 * A_WIDTH, A_WIDTH, 2 * A_HEADS, 2 * A_HEADS, B_WIDTH, B_WIDTH, B_WIDTH, C_WIDTH, C_WIDTH, C_WIDTH, N_BRANCHES * D_MODEL)
IN_COLS = sum(SPLIT_SIZES)

kernel_name = 'hybrid_bidir_gdn_shortconv_dilated_encoder'


def rms_norm(x, w):
    x32 = x.astype(jnp.float32)
    y = x32 * lax.rsqrt(jnp.mean(x32 * x32, axis=-1, keepdims=True) + EPS)
    return (y * w.astype(jnp.float32)).astype(x.dtype)


def l2_normalize(x):
    return x * lax.rsqrt(jnp.sum(x * x, axis=-1, keepdims=True) + EPS)


def dwconv_centred(x, w):
    k = w.shape[0]
    r = k // 2
    s = x.shape[1]
    xp = jnp.pad(x, ((0, 0), (r, r), (0, 0)))
    y = xp[:, 0:s] * w[0]
    for j in range(1, k):
        y = y + xp[:, j:j + s] * w[j]
    return y


def gated_delta_rule_chunked(q, k, v, g, beta):
    b, h, s, dk = q.shape
    dv = v.shape[-1]
    c = GDN_CHUNK
    n = s // c
    q, k, v = (t.reshape(b, h, n, c, t.shape[-1]) for t in (q, k, v))
    gc = jnp.cumsum(g.reshape(b, h, n, c), axis=-1)
    beta = beta.reshape(b, h, n, c, 1)
    incl = jnp.tril(jnp.ones((c, c), dtype=bool))
    strict = jnp.tril(jnp.ones((c, c), dtype=bool), -1)
    decay = jnp.exp(jnp.where(incl, gc[..., :, None] - gc[..., None, :], -jnp.inf))
    kb = k * beta
    vb = v * beta
    lower = jnp.where(strict, jnp.einsum('bhncd,bhnsd->bhncs', kb, k) * decay, 0.0)
    t_mat = jnp.eye(c, dtype=q.dtype) + lower
    u = lax.linalg.triangular_solve(t_mat, vb, left_side=True, lower=True, unit_diagonal=True)
    w = lax.linalg.triangular_solve(t_mat, kb * jnp.exp(gc)[..., None], left_side=True, lower=True, unit_diagonal=True)
    intra = jnp.where(incl, jnp.einsum('bhncd,bhnsd->bhncs', q, k) * decay, 0.0)

    def step(state, xs):
        q_i, k_i, u_i, w_i, gc_i, a_i = xs
        v_new = u_i - jnp.einsum('bhcd,bhde->bhce', w_i, state)
        o_i = (jnp.einsum('bhcd,bhde->bhce', q_i * jnp.exp(gc_i)[..., None], state)
               + jnp.einsum('bhcs,bhse->bhce', a_i, v_new))
        g_last = gc_i[..., -1:]
        k_dec = k_i * jnp.exp(g_last - gc_i)[..., None]
        state = state * jnp.exp(g_last)[..., None] + jnp.einsum('bhcd,bhce->bhde', k_dec, v_new)
        return state, o_i

    xs = tuple(jnp.moveaxis(t, 2, 0) for t in (q, k, u, w, gc, intra))
    state0 = jnp.zeros((b, h, dk, dv), q.dtype)
    _, o = lax.scan(step, state0, xs)
    return jnp.moveaxis(o, 0, 2).reshape(b, h, s, dv)


def gdn_mixer(qkv, z, a, bt, a_log, dt_bias, head_norm):
    dtype = z.dtype
    bsz, s, _ = qkv.shape
    f32 = jnp.float32
    qkv = jax.nn.silu(qkv.astype(f32))
    q, k, v = (t.reshape(bsz, s, A_HEADS, A_HEAD_DIM).transpose(0, 2, 1, 3) for t in jnp.split(qkv, 3, axis=-1))
    q = l2_normalize(q) * (A_HEAD_DIM ** -0.5)
    k = l2_normalize(k)
    a = a.astype(f32).reshape(bsz, s, 2, A_HEADS)
    g = -jnp.exp(a_log.astype(f32)) * jax.nn.softplus(a + dt_bias.astype(f32))
    beta = jax.nn.sigmoid(bt.astype(f32).reshape(bsz, s, 2, A_HEADS))
    g = g.transpose(2, 0, 3, 1)
    beta = beta.transpose(2, 0, 3, 1)
    o_fwd = gated_delta_rule_chunked(q, k, v, g[0], beta[0])
    rev = lambda t: jnp.flip(t, axis=2)
    o_bwd = rev(gated_delta_rule_chunked(rev(q), rev(k), rev(v), rev(g[1]), rev(beta[1])))
    o = (o_fwd + o_bwd).transpose(0, 2, 1, 3)
    o = o * lax.rsqrt(jnp.mean(o * o, axis=-1, keepdims=True) + EPS) * head_norm.astype(f32)
    o = o * jax.nn.silu(z.astype(f32).reshape(bsz, s, A_HEADS, A_HEAD_DIM))
    return o.reshape(bsz, s, A_WIDTH).astype(dtype)


def t5_bucket(rel):
    half = N_BUCKETS // 2
    exact = half // 2
    ret = np.where(rel > 0, half, 0)
    n = np.abs(rel)
    large = exact + (np.log(np.maximum(n, 1) / exact) / math.log(REL_MAX_DIST / exact) * (half - exact)).astype(np.int32)
    large = np.minimum(large, half - 1)
    return (ret + np.where(n < exact, n, large)).astype(np.int32)


def banded_attention(q, k, v, bias_tab, radius, dil):
    n, l, h, dh = q.shape
    blk = radius
    nb = -(-l // blk)
    pad = nb * blk - l
    qb = jnp.pad(q, ((0, 0), (0, pad), (0, 0), (0, 0))).reshape(n, nb, blk, h, dh)

    def key_blocks(t):
        tp = jnp.pad(t, ((0, 0), (blk, pad + blk), (0, 0), (0, 0)))
        return jnp.concatenate([tp[:, j * blk:j * blk + nb * blk].reshape(n, nb, blk, h, dh) for j in range(3)], axis=2)

    kb = key_blocks(k)
    vb = key_blocks(v)
    rel0 = np.arange(3 * blk)[None, :] - blk - np.arange(blk)[:, None]
    bias = bias_tab[t5_bucket(rel0 * dil)].transpose(2, 0, 1)
    kpos = (np.arange(nb)[:, None, None] - 1) * blk + np.arange(3 * blk)[None, None, :]
    valid = (np.abs(rel0)[None] <= radius) & (kpos >= 0) & (kpos < l)
    sc = jnp.einsum('nbqhd,nbkhd->nbhqk', qb, kb) * (dh ** -0.5) + bias[None, None]
    sc = jnp.where(valid[None, :, None], sc, NEG_INF)
    m = jnp.max(sc, axis=-1, keepdims=True)
    p = jnp.exp(sc - m)
    den = jnp.sum(p, axis=-1, keepdims=True)
    o = jnp.einsum('nbhqk,nbkhd->nbqhd', p / den, vb).reshape(n, nb * blk, h, dh)[:, :l]
    lse = (m + jnp.log(den))[..., 0].transpose(0, 1, 3, 2).reshape(n, nb * blk, h)[:, :l]
    return o, lse


def dilated_mixer(q, k, v, rel_bias):
    dtype = q.dtype
    bsz, s, _ = q.shape
    shp = (bsz, s, C_HEADS, C_HEAD_DIM)
    q, k, v = (t.astype(jnp.float32).reshape(shp) for t in (q, k, v))
    outs, lses = [], []
    for gi, (window, dil) in enumerate(C_GROUPS):
        hs = slice(gi * C_HEADS_PER_GROUP, (gi + 1) * C_HEADS_PER_GROUP)
        l = s // dil

        def to_sub(t):
            return t[:, :, hs].reshape(bsz, l, dil, C_HEADS_PER_GROUP, C_HEAD_DIM).transpose(0, 2, 1, 3, 4).reshape(bsz * dil, l, C_HEADS_PER_GROUP, C_HEAD_DIM)

        o, lse = banded_attention(to_sub(q), to_sub(k), to_sub(v), rel_bias[:, hs].astype(jnp.float32), window // (2 * dil), dil)
        outs.append(o.reshape(bsz, dil, l, C_HEADS_PER_GROUP, C_HEAD_DIM).transpose(0, 2, 1, 3, 4).reshape(bsz, s, C_HEADS_PER_GROUP, C_HEAD_DIM))
        lses.append(lse.reshape(bsz, dil, l, C_HEADS_PER_GROUP).transpose(0, 2, 1, 3).reshape(bsz, s, C_HEADS_PER_GROUP))
    alpha = jax.nn.softmax(jnp.stack(lses), axis=0)
    o = jnp.einsum('gbsh,gbshd->bshd', alpha, jnp.stack(outs))
    return o.reshape(bsz, s, C_OUT).astype(dtype)


def trunk(x, rel_bias, norm_mix, w_in, conv_a, a_log, dt_bias, head_norm, conv_b,
          w_br_a, w_br_b, w_br_c, w_out, norm_mlp, w_up, w_down, norm_final):
    split_points = np.cumsum(SPLIT_SIZES)[:-1].tolist()
    h = x
    for layer in range(DEPTH):
        xn = rms_norm(h, norm_mix[layer])
        proj = xn @ w_in[layer]
        (a_qkv, a_z, a_alpha, a_beta, b_gb, b_gc, b_h, c_q, c_k, c_v, gates) = jnp.split(proj, split_points, axis=-1)
        y_a = gdn_mixer(dwconv_centred(a_qkv, conv_a[layer]), a_z, a_alpha, a_beta,
                        a_log[layer], dt_bias[layer], head_norm[layer])
        y_b = b_gb * dwconv_centred(b_gc * b_h, conv_b[layer])
        y_c = dilated_mixer(c_q, c_k, c_v, rel_bias)
        g_a, g_b, g_c = jnp.split(jax.nn.sigmoid(gates), N_BRANCHES, axis=-1)
        merged = g_a * (y_a @ w_br_a[layer]) + g_b * (y_b @ w_br_b[layer]) + g_c * (y_c @ w_br_c[layer])
        h = h + merged @ w_out[layer]
        xn = rms_norm(h, norm_mlp[layer])
        h = h + jnp.square(jax.nn.relu(xn @ w_up[layer])) @ w_down[layer]
    return rms_norm(h, norm_final)


def setup_inputs(seed: int = 0) -> dict:
    key = jax.random.key(seed)
    ks = jax.random.split(key, 20)
    f32 = jnp.float32

    def normal(k, shape, scale):
        return jax.random.normal(k, shape, f32) * scale

    def gain(k, shape):
        return 1.0 + 0.02 * jax.random.normal(k, shape, f32)

    dt = jnp.exp(jax.random.uniform(ks[6], (DEPTH, 2, A_HEADS), f32, math.log(1e-3), math.log(1e-1)))
    return {
        'x_prompt': jax.random.normal(ks[0], (BATCH, SEQ, D_MODEL), f32),
        'x_sample': jax.random.normal(ks[1], (DEC_BATCH, DEC_SEQ, D_MODEL), f32),
        'rel_bias': normal(ks[2], (N_BUCKETS, C_HEADS), 0.2),
        'norm_mix': gain(ks[3], (DEPTH, D_MODEL)),
        'w_in': normal(ks[4], (DEPTH, D_MODEL, IN_COLS), D_MODEL ** -0.5),
        'conv_a': normal(ks[5], (DEPTH, A_CONV, 3 * A_WIDTH), A_CONV ** -0.5),
        'a_log': jnp.log(jax.random.uniform(ks[7], (DEPTH, 2, A_HEADS), f32, 1.0, 16.0)),
        'dt_bias': dt + jnp.log(-jnp.expm1(-dt)),
        'head_norm': gain(ks[8], (DEPTH, A_HEAD_DIM)),
        'conv_b': normal(ks[9], (DEPTH, B_CONV, B_WIDTH), B_CONV ** -0.5),
        'w_br_a': normal(ks[10], (DEPTH, A_WIDTH, D_MODEL), A_WIDTH ** -0.5),
        'w_br_b': normal(ks[11], (DEPTH, B_WIDTH, D_MODEL), B_WIDTH ** -0.5),
        'w_br_c': normal(ks[12], (DEPTH, C_OUT, D_MODEL), C_OUT ** -0.5),
        'w_out': normal(ks[13], (DEPTH, D_MODEL, D_MODEL), D_MODEL ** -0.5),
        'norm_mlp': gain(ks[14], (DEPTH, D_MODEL)),
        'w_up': normal(ks[15], (DEPTH, D_MODEL, D_FF), D_MODEL ** -0.5),
        'w_down': normal(ks[16], (DEPTH, D_FF, D_MODEL), D_FF ** -0.5),
        'norm_final': gain(ks[17], (D_MODEL,)),
    }


def reference(x_prompt, x_sample, rel_bias, norm_mix, w_in, conv_a, a_log, dt_bias, head_norm, conv_b,
              w_br_a, w_br_b, w_br_c, w_out, norm_mlp, w_up, w_down, norm_final):
    y_prompt = trunk(x_prompt, rel_bias, norm_mix, w_in, conv_a, a_log, dt_bias, head_norm, conv_b,
                     w_br_a, w_br_b, w_br_c, w_out, norm_mlp, w_up, w_down, norm_final)
    y_sample = trunk(x_sample, rel_bias, norm_mix, w_in, conv_a, a_log, dt_bias, head_norm, conv_b,
                     w_br_a, w_br_b, w_br_c, w_out, norm_mlp, w_up, w_down, norm_final)
    return (y_prompt, y_sample)
```

```python
import numpy as np
from contextlib import ExitStack
import concourse.bass as bass
import concourse.mybir as mybir
from concourse.bass_utils import run_bass_kernel_spmd

F32 = mybir.dt.float32
BF16 = mybir.dt.bfloat16
ALU = mybir.AluOpType
AF = mybir.ActivationFunctionType
AX = mybir.AxisListType

D = 2048
DEPTH = 2
EPS = 1e-6


class Prog:
    ENGS = ('pe', 'dve', 'act', 'pool', 'sp')
    NCH = 24
    EPOCH = 30000

    def __init__(self, nc):
        self.nc = nc
        self.ops = []
        self.last_w = {}
        self.readers = {}
        self._bar = 0
        self.marks = []

    def add(self, eng, fn, reads=(), writes=(), dma=False):
        idx = len(self.ops)
        writes = list(writes) + [k for k in reads if isinstance(k, str) and k.startswith('ps')]
        reads = [k for k in reads if not (isinstance(k, str) and k.startswith('ps'))]
        deps = set()
        for k in reads:
            if k in self.last_w:
                deps.add(self.last_w[k])
        for k in writes:
            if k in self.last_w:
                deps.add(self.last_w[k])
            deps.update(self.readers.get(k, ()))
        for k in reads:
            self.readers.setdefault(k, []).append(idx)
        for k in writes:
            self.last_w[k] = idx
            self.readers[k] = []
        deps.discard(idx)
        self.ops.append(dict(eng=eng, fn=fn, deps=deps, dma=dma, sig=dma))
        return idx

    def barrier(self):
        deps = set()
        lastc = {}
        for i in range(len(self.ops) - 1, self._bar - 1, -1):
            o = self.ops[i]
            if o['dma']:
                deps.add(i)
            elif o['eng'] not in lastc:
                lastc[o['eng']] = i
                deps.add(i)
        for i in range(self._bar - 1, -1, -1):
            o = self.ops[i]
            if not o['dma'] and o['eng'] not in lastc:
                lastc[o['eng']] = i
                deps.add(i)
            if len(lastc) == 4:
                break
        self.last_w = {}
        self.readers = {}
        for e in self.ENGS:
            self.ops.append(dict(eng=e, fn=lambda eng: None, deps=set(deps), dma=False, sig=False, bar=True))
        self._bar = len(self.ops)

    def pe(self, fn, r=(), w=()):
        return self.add('pe', fn, r, w)

    def dve(self, fn, r=(), w=()):
        return self.add('dve', fn, r, w)

    def act(self, fn, r=(), w=()):
        return self.add('act', fn, r, w)

    def pool(self, fn, r=(), w=()):
        return self.add('pool', fn, r, w)

    def dma(self, fn, r=(), w=(), q='sp'):
        return self.add(q, fn, r, w, dma=True)

    def emit(self, es):
        nc = self.nc
        ops = self.ops
        pos = {e: 0 for e in self.ENGS}
        for o in ops:
            o['pos'] = pos[o['eng']]
            pos[o['eng']] += 1
        for i, o in enumerate(ops):
            need = set()
            for d in o['deps']:
                od = ops[d]
                if od['eng'] == o['eng'] and not od['dma'] and not o.get('bar'):
                    if o['eng'] == 'pe':
                        continue
                    if o['eng'] in ('sp',):
                        continue
                    if o['pos'] - od['pos'] > 3 and not o['dma']:
                        continue
                need.add(d)
                od['sig'] = True
            o['need'] = need
        cnt = {e: 0 for e in self.ENGS}
        sems = {}

        def getsem(name):
            if name not in sems:
                sems[name] = es.enter_context(nc.semaphore(name))
            return sems[name]

        dcount = {e: 0 for e in self.ENGS}
        chcount = {}
        for o in ops:
            e = o['eng']
            if o['dma']:
                ch = dcount[e] % self.NCH
                dcount[e] += 1
                k = chcount.get((e, ch), 0)
                o['prev_ev'] = (getsem(f"d_{e}_{ch}"), 16 * k) if k > 0 else None
                chcount[(e, ch)] = k + 1
                assert 16 * (k + 1) < 32000, "too many DMAs per channel"
                o['ev'] = (getsem(f"d_{e}_{ch}"), 16 * (k + 1))
            elif o['sig']:
                c = cnt[e]
                ep = c // self.EPOCH
                o['ev'] = (getsem(f"c_{e}_{ep}"), c % self.EPOCH + 1)
                cnt[e] += 1
            else:
                o['ev'] = None
        by_eng = {e: [o for o in ops if o['eng'] == e] for e in self.ENGS}
        self.stats = {e: len(v) for e, v in by_eng.items()}

        def run(engname, eng):
            waited = {}
            for o in by_eng[engname]:
                evs = [ops[d]['ev'] for d in sorted(o['need'])]
                if o['dma'] and o['prev_ev'] is not None:
                    evs.append(o['prev_ev'])
                mx = {}
                for (s, v) in evs:
                    if v > mx.get(id(s), (None, 0))[1]:
                        mx[id(s)] = (s, v)
                for key, (s, v) in mx.items():
                    if waited.get(key, 0) >= v:
                        continue
                    eng.wait_ge(s, v)
                    waited[key] = v
                ins = o['fn'](eng)
                if o['ev'] is not None:
                    if ins is None:
                        ins = eng.nop()
                    ins.then_inc(o['ev'][0], 16 if o['dma'] else 1)

        with nc.Block() as block:
            @block.tensor
            def _(eng):
                run('pe', eng)

            @block.vector
            def _(eng):
                run('dve', eng)

            @block.scalar
            def _(eng):
                run('act', eng)

            @block.gpsimd
            def _(eng):
                run('pool', eng)

            @block.sync
            def _(eng):
                run('sp', eng)


A_W = 1024
B_W = 1024
C_W = 1536
IN_COLS = 17952
OFF_Q, OFF_K, OFF_V, OFF_Z, OFF_AB = 0, 1024, 2048, 3072, 4096
OFF_GB, OFF_GC, OFF_H = 4128, 5152, 6176
OFF_CQ, OFF_CK, OFF_CV, OFF_G = 7200, 8736, 10272, 11808
D_FF = 8192


class Cfg:
    def __init__(self, T=4096, depth=DEPTH, debug=(), stop=None):
        self.T = T
        self.depth = depth
        self.debug = tuple(debug)
        self.stop = stop


def build(cfg):
    T = cfg.T
    NT = T // 128
    TB = T // 2
    L = cfg.depth
    nc = bass.Bass("TRN2", target_bir_lowering=False)
    es = ExitStack()
    P = Prog(nc)
    dbg = set(cfg.debug)
    outs = []

    def dram(name, shape, dt, kind="Internal"):
        if name in dbg:
            kind = "ExternalOutput"
            outs.append(name)
        return nc.dram_tensor(name, shape, dt, kind=kind).ap()

    arena = es.enter_context(nc.sbuf_tensor("arena", [128, 52992], F32))
    psum = es.enter_context(nc.psum_tensor("psum", [128, 4096], F32))

    class Arena:
        def __init__(self):
            self.off = 0
            self.n = 0

        def alloc(self, free, dt, name=None):
            if isinstance(free, int):
                free = [free]
            n = 1
            for f in free:
                n *= f
            words = (n + 1) // 2 if dt == BF16 else n
            words = (words + 7) // 8 * 8
            assert self.off + words <= 52992, f"arena overflow {self.off + words}"
            ap = arena[:, self.off:self.off + words]
            self.off += words
            if dt == BF16:
                ap = ap.bitcast(BF16)
            ap = ap[:, 0:n]
            if len(free) == 2:
                ap = ap.rearrange("p (a b) -> p a b", a=free[0])
            elif len(free) == 3:
                ap = ap.rearrange("p (a b c) -> p a b c", a=free[0], b=free[1])
            self.n += 1
            return ap

    AR = Arena()

    def bank(i, dt=F32):
        ap = psum[:, i * 512:(i + 1) * 512]
        if dt == BF16:
            ap = ap.bitcast(BF16)
        return ap

    x_in = dram("x", [T, D], F32, "ExternalInput")
    keep_in = dram("keep", [128, 1], F32, "ExternalInput")
    consts_in = dram("consts", [128, 9, 128], F32, "ExternalInput")
    w_in = [dram(f"w_in{l}", [D, IN_COLS], F32, "ExternalInput") for l in range(DEPTH)]
    w_br_a = dram("w_br_a", [DEPTH, A_W, D], F32, "ExternalInput")
    w_br_b = dram("w_br_b", [DEPTH, B_W, D], F32, "ExternalInput")
    w_br_c = dram("w_br_c", [DEPTH, 512, D], F32, "ExternalInput")
    w_out = dram("w_out", [DEPTH, D, D], F32, "ExternalInput")
    w_up = dram("w_up", [DEPTH, D, D_FF], F32, "ExternalInput")
    w_down = dram("w_down", [DEPTH, D_FF, D], F32, "ExternalInput")
    norms_in = dram("norms", [2 * DEPTH + 1, D], F32, "ExternalInput")
    conva_in = dram("conva", [DEPTH, 128, 72], F32, "ExternalInput")
    convb_in = dram("convb", [DEPTH, 128, 24], F32, "ExternalInput")
    gdnp_in = dram("gdnp", [DEPTH, 32], F32, "ExternalInput")
    hn_in = dram("hn", [DEPTH, 128], F32, "ExternalInput")
    relb_in = dram("relb", [32, 12], F32, "ExternalInput")
    eoh_in = dram("eoh", [32, 3, 768], F32, "ExternalInput")
    amask_in = dram("amask", [128, 3, 128], F32, "ExternalInput")
    jmat_in = dram("jmat", [128, 128], F32, "ExternalInput")
    y_out = dram("y", [T, D], F32, "ExternalOutput")

    h_d = dram("h_d", [T, D], F32)
    xnT_d = dram("xnT_d", [16, 128, T + 2], BF16)
    qT_d = dram("qT_d", [8, 128, T], BF16)
    kT_d = dram("kT_d", [8, 128, T], BF16)
    vT_d = dram("vT_d", [8, 128, T], BF16)
    z_d = dram("z_d", [T, A_W], F32)
    ybT_d = dram("ybT_d", [8, 128, T], BF16)
    cqT_d = dram("cqT_d", [12, 128, T], BF16)
    ckT_d = dram("ckT_d", [12, 128, T], BF16)
    cv_d = dram("cv_d", [T, C_W], BF16)
    yaT_d = dram("yaT_d", [8, 128, T], BF16)
    ycT_d = dram("ycT_d", [4, 128, T], BF16)
    ab_d = dram("ab_d", [T, 32], F32)

    cst = AR.alloc([9, 128], F32)
    ident_f = cst[:, 0, :]
    ident_b = AR.alloc(128, BF16)
    ones_b = AR.alloc(128, BF16)
    ones_f = AR.alloc(128, F32)
    keepc = AR.alloc(1, F32)
    epsc = AR.alloc(1, F32)
    zero_b = AR.alloc(16, BF16)
    P.dma(lambda e: e.dma_start(out=cst, in_=consts_in), w=['cst'])
    P.dma(lambda e: e.dma_start(out=keepc, in_=keep_in), w=['keepc'])
    P.dve(lambda e: e.tensor_copy(ident_b, ident_f), r=['cst'], w=['ident_b'])
    P.dve(lambda e: e.memset(ones_b, 1.0), w=['ones_b'])
    P.dve(lambda e: e.memset(ones_f, 1.0), w=['ones_f'])
    P.dve(lambda e: e.memset(epsc, EPS), w=['epsc'])
    P.dve(lambda e: e.memset(zero_b, 0.0), w=['zero_b'])
    for col in (0, T + 1):
        P.dma(lambda e, col=col: e.dma_start(out=xnT_d[:, :, col:col + 1].rearrange("c p t -> p c t"),
                                             in_=zero_b.rearrange("p (c t) -> p c t", t=1), allow_slow_non_contiguous=True),
              r=['zero_b'], w=[('xnT_pad', col)])
    PERSIST = AR.off

    def wkey(i):
        return f'wt{i}'

    wcount = [0]

    def phase_norm(src_ap, nrow, final_out=None):
        P.marks.append(('phase_norm', len(P.ops)))
        AR.off = PERSIST
        nw = AR.alloc(D, F32)
        hbuf = [AR.alloc(D, F32) for _ in range(3)]
        sq = AR.alloc(D, F32)
        ssum = [AR.alloc(1, F32) for _ in range(3)]
        rstd = [AR.alloc(1, F32) for _ in range(3)]
        xnb = [AR.alloc(D, BF16) for _ in range(3)]
        xnTt = [AR.alloc([16, 128], BF16) for _ in range(3)]
        yo = [AR.alloc(D, F32) for _ in range(3)]
        P.dma(lambda e: e.dma_start(out=nw, in_=norms_in[nrow:nrow + 1, :].partition_broadcast(128)), w=['nw'])
        def ld(t):
            b = t % 3
            P.dma(lambda e, t=t, b=b: e.dma_start(out=hbuf[b], in_=src_ap[t * 128:(t + 1) * 128, :]),
                  w=[f'hbuf{b}'])
        ld(0)
        if NT > 1:
            ld(1)
        for t in range(NT):
            b = t % 3
            if t + 2 < NT:
                ld(t + 2)
            P.act(lambda e, b=b: e.activation(out=sq, in_=hbuf[b], func=AF.Square, accum_out=ssum[b]),
                  r=[f'hbuf{b}'], w=['sq', f'ssum{b}'])
            P.act(lambda e, b=b: e.activation(out=ssum[b], in_=ssum[b], func=AF.Sqrt, scale=1.0 / D, bias=epsc),
                  r=[f'ssum{b}', 'epsc'], w=[f'ssum{b}'])
            P.dve(lambda e, b=b: e.reciprocal(rstd[b], ssum[b]), r=[f'ssum{b}'], w=[f'rstd{b}'])
            if final_out is not None:
                P.dve(lambda e, b=b: e.scalar_tensor_tensor(out=yo[b], in0=hbuf[b], scalar=rstd[b], in1=nw,
                                                            op0=ALU.mult, op1=ALU.mult),
                      r=[f'hbuf{b}', f'rstd{b}', 'nw'], w=[f'yo{b}'])
                P.dma(lambda e, t=t, b=b: e.dma_start(out=final_out[t * 128:(t + 1) * 128, :], in_=yo[b]),
                      r=[f'yo{b}'], w=[('y_out', t)], q='pool')
                continue
            P.dve(lambda e, b=b: e.scalar_tensor_tensor(out=xnb[b], in0=hbuf[b], scalar=rstd[b], in1=nw,
                                                        op0=ALU.mult, op1=ALU.mult),
                  r=[f'hbuf{b}', f'rstd{b}', 'nw'], w=[f'xnb{b}'])
            for half in range(2):
                pb = bank(6 + half, BF16).rearrange("p (a b) -> p a b", a=8)
                for j in range(8):
                    c = half * 8 + j
                    P.pe(lambda e, b=b, c=c, j=j, pb=pb: e.transpose(pb[:, j, :], xnb[b][:, c * 128:(c + 1) * 128], ident_b),
                         r=[f'xnb{b}', 'ident_b'], w=[f'ps{6 + half}'])
                if half == 0:
                    P.dve(lambda e, b=b, pb=pb: e.tensor_copy(xnTt[b][:, 0:8, :], pb), r=['ps6'], w=[f'xnTt{b}a'])
                else:
                    P.act(lambda e, b=b, pb=pb: e.copy(xnTt[b][:, 8:16, :], pb), r=['ps7'], w=[f'xnTt{b}b'])
            P.dma(lambda e, t=t, b=b: e.dma_start(out=xnT_d[:, :, 1 + t * 128:1 + (t + 1) * 128].rearrange("c p t -> p c t"),
                                                  in_=xnTt[b]),
                  r=[f'xnTt{b}a', f'xnTt{b}b'], w=[('xnT_d', t)], q='pool')
        P.barrier()

    def load_xb(xb, tb):
        P.dma(lambda e: e.dma_start(out=xb, in_=xnT_d[:, :, tb * TB:tb * TB + TB + 2].rearrange("c p t -> p c t")),
              w=['xb'])
        col = TB + 1 if tb == 0 else 0
        P.dve(lambda e: e.tensor_scalar(xb[:, :, col:col + 1], xb[:, :, col:col + 1], keepc, None, ALU.mult),
              r=['xb', 'keepc'], w=['xb'])

    def load_w(wt, src2d, ncols, i):
        kc = src2d.shape[0] // 128
        P.dma(lambda e: e.dma_start(out=wt[:, 0:kc, 0:ncols], in_=src2d.rearrange("(c p) n -> p c n", p=128)),
              w=[wkey(i)], q='pool')

    def phase_proj(l):
        P.marks.append(('phase_proj', len(P.ops)))
        AR.off = PERSIST
        xb = AR.alloc([16, TB + 2], BF16)
        wts = [AR.alloc([16, 512], BF16) for _ in range(2)]
        cwa = AR.alloc(72, F32)
        cwb = AR.alloc(24, F32)
        st = [AR.alloc(512, F32) for _ in range(3)]
        sbf = [AR.alloc(512, BF16) for _ in range(2)]
        rn = AR.alloc(512, F32)
        sqb2 = [AR.alloc(512, BF16) for _ in range(2)]
        st1b = AR.alloc(512, F32)
        ab_sb = AR.alloc(32, F32)
        P.dma(lambda e: e.dma_start(out=cwa, in_=conva_in[l]), w=['cwa'])
        P.dma(lambda e: e.dma_start(out=cwb, in_=convb_in[l]), w=['cwb'])
        W = w_in[l]
        tiles = []
        s = 0
        while s < TB:
            n = min(510, TB - s)
            tiles.append((s, n))
            s += n
        oc = [0]
        pend = []
        pcnt = [0]

        def nxt_w():
            i = wcount[0] % 2
            wcount[0] += 1
            return i

        def fm_group(wi, wcol, x0, n, bk):
            for kc in range(16):
                P.pe(lambda e, kc=kc: e.matmul(bank(bk)[:, 0:n], wts[wi][:, kc, wcol:wcol + 128], xb[:, kc, x0:x0 + n],
                                               start=(kc == 0), stop=(kc == 15)),
                     r=[wkey(wi), 'xb'], w=[f'ps{bk}'])

        def out_dma(dst, src, rkey):
            P.dma(lambda e: e.dma_start(out=dst, in_=src), r=[rkey], w=[('pout', oc[0])])
            oc[0] += 1

        for tb in range(2):
            load_xb(xb, tb)
            t0 = tb * TB
            for wtile in range(6):
                wi = nxt_w()
                load_w(wts[wi], W[:, wtile * 512:(wtile + 1) * 512], 512, wi)
                for cg in range(4):
                    g = wtile * 4 + cg
                    kind, hd = g // 8, g % 8
                    for ti, (s, n) in enumerate(tiles):
                        bk = (g * len(tiles) + ti) % 4
                        fm_group(wi, cg * 128, s, n + 2, bk)
                        pre = bank(bk)
                        a = st[0]
                        P.dve(lambda e, pre=pre, n=n, g=g: e.tensor_scalar(a[:, 0:n], pre[:, 0:n], cwa[:, 3 * g:3 * g + 1], None, ALU.mult),
                              r=[f'ps{bk}', 'cwa'], w=['st0'])
                        P.dve(lambda e, pre=pre, n=n, g=g: e.scalar_tensor_tensor(out=a[:, 0:n], in0=pre[:, 1:n + 1], scalar=cwa[:, 3 * g + 1:3 * g + 2],
                                                                              in1=a[:, 0:n], op0=ALU.mult, op1=ALU.add),
                              r=[f'ps{bk}', 'cwa', 'st0'], w=['st0'])
                        P.dve(lambda e, pre=pre, n=n, g=g: e.scalar_tensor_tensor(out=a[:, 0:n], in0=pre[:, 2:n + 2], scalar=cwa[:, 3 * g + 2:3 * g + 3],
                                                                              in1=a[:, 0:n], op0=ALU.mult, op1=ALU.add),
                              r=[f'ps{bk}', 'cwa', 'st0'], w=['st0'])
                        ob = oc[0] % 2
                        dst = (qT_d, kT_d, vT_d)[kind][hd][:, t0 + s:t0 + s + n]
                        if kind == 2:
                            while pend:
                                pend.pop(0)()
                        elif len(pend) == 2:
                            pend.pop(0)()
                        if kind == 2:
                            P.act(lambda e, n=n, ob=ob: e.activation(out=sbf[ob][:, 0:n], in_=a[:, 0:n], func=AF.Silu),
                                  r=['st0'], w=[f'sbf{ob}'])
                        else:
                            pj = pcnt[0] % 2
                            pcnt[0] += 1
                            b1 = (st[1], st1b)[pj]
                            sqb = sqb2[pj]
                            P.act(lambda e, n=n, b1=b1: e.activation(out=b1[:, 0:n], in_=a[:, 0:n], func=AF.Silu),
                                  r=['st0'], w=[('st1', 'st1b')[pj]])
                            P.act(lambda e, n=n, b1=b1, sqb=sqb: e.activation(out=sqb[:, 0:n], in_=b1[:, 0:n], func=AF.Square),
                                  r=[('st1', 'st1b')[pj]], w=[f'sqb{pj}'])
                            def part2(n=n, ob=ob, kind=kind, dst=dst, b1=b1, sqb=sqb, pj=pj):
                                P.pe(lambda e: e.matmul(bank(4)[:, 0:n], ones_b, sqb[:, 0:n], start=True, stop=True),
                                     r=[f'sqb{pj}', 'ones_b'], w=['ps4'])
                                P.act(lambda e: e.activation(out=rn[:, 0:n], in_=bank(4)[:, 0:n], func=AF.Sqrt, bias=epsc),
                                      r=['ps4', 'epsc'], w=['rn'])
                                P.dve(lambda e: e.reciprocal(rn[:, 0:n], rn[:, 0:n]), r=['rn'], w=['rn'])
                                sc = (128.0 ** -0.5) if kind == 0 else 1.0
                                P.dve(lambda e: e.scalar_tensor_tensor(out=sbf[ob][:, 0:n], in0=b1[:, 0:n], scalar=sc, in1=rn[:, 0:n],
                                                                       op0=ALU.mult, op1=ALU.mult),
                                      r=[('st1', 'st1b')[pj], 'rn'], w=[f'sbf{ob}'])
                                P.dma(lambda e: e.dma_start(out=dst, in_=sbf[ob][:, 0:n]), r=[f'sbf{ob}'], w=[('pout2', n, ob, id(dst))])
                            pend.append(part2)
                            oc[0] += 1
                            continue
                        out_dma(dst, sbf[ob][:, 0:n], f'sbf{ob}')
            while pend:
                pend.pop(0)()
            for wtile in range(2):
                wi = nxt_w()
                load_w(wts[wi], W[:, OFF_Z + wtile * 512:OFF_Z + (wtile + 1) * 512], 512, wi)
                for t in range(TB // 128):
                    bk = t % 4
                    for kc in range(16):
                        P.pe(lambda e, kc=kc, t=t, bk=bk, wi=wi: e.matmul(bank(bk), xb[:, kc, 1 + t * 128:1 + (t + 1) * 128], wts[wi][:, kc, :],
                                                                   start=(kc == 0), stop=(kc == 15)),
                             r=[wkey(wi), 'xb'], w=[f'ps{bk}'])
                    sj = 1 + t % 2
                    P.act(lambda e, bk=bk, sj=sj: e.activation(out=st[sj], in_=bank(bk), func=AF.Silu),
                          r=[f'ps{bk}'], w=[f'st{sj}'])
                    out_dma(z_d[t0 + t * 128:t0 + (t + 1) * 128, wtile * 512:(wtile + 1) * 512], st[sj], f'st{sj}')
            wi = nxt_w()
            load_w(wts[wi], W[:, OFF_AB:OFF_AB + 32], 32, wi)
            for t in range(TB // 128):
                bk = t % 4
                for kc in range(16):
                    P.pe(lambda e, kc=kc, t=t, bk=bk, wi=wi: e.matmul(bank(bk)[:, 0:32], xb[:, kc, 1 + t * 128:1 + (t + 1) * 128], wts[wi][:, kc, 0:32],
                                                               start=(kc == 0), stop=(kc == 15)),
                         r=[wkey(wi), 'xb'], w=[f'ps{bk}'])
                P.act(lambda e, bk=bk: e.copy(ab_sb, bank(bk)[:, 0:32]), r=[f'ps{bk}'], w=['ab_sb'])
                out_dma(ab_d[t0 + t * 128:t0 + (t + 1) * 128, :], ab_sb, 'ab_sb')
            for j in range(8):
                wi = nxt_w()
                for q, off in enumerate((OFF_GB, OFF_GC, OFF_H)):
                    P.dma(lambda e, q=q, off=off, wi=wi, j=j: e.dma_start(out=wts[wi][:, :, q * 128:(q + 1) * 128],
                                                                     in_=W[:, off + j * 128:off + (j + 1) * 128].rearrange("(c p) n -> p c n", p=128)),
                          w=[wkey(wi)], q='pool')
                for ti, (s, n) in enumerate(tiles):
                    bg, bc, bh = ((0, 1, 2), (5, 6, 7))[(j * len(tiles) + ti) % 2]
                    fm_group(wi, 256, s, n + 2, bh)
                    fm_group(wi, 128, s, n + 2, bc)
                    fm_group(wi, 0, s, n + 2, bg)
                    hh = st[1]
                    pp = st[2]
                    a = st[0]
                    P.act(lambda e, n=n, bh=bh: e.copy(hh[:, 0:n + 2], bank(bh)[:, 0:n + 2]), r=[f'ps{bh}'], w=['st1'])
                    P.dve(lambda e, n=n, bc=bc: e.tensor_tensor(out=pp[:, 0:n + 2], in0=bank(bc)[:, 0:n + 2], in1=hh[:, 0:n + 2], op=ALU.mult),
                          r=[f'ps{bc}', 'st1'], w=['st2'])
                    P.dve(lambda e, n=n, j=j: e.tensor_scalar(a[:, 0:n], pp[:, 0:n], cwb[:, 3 * j:3 * j + 1], None, ALU.mult),
                          r=['st2', 'cwb'], w=['st0'])
                    P.dve(lambda e, n=n, j=j: e.scalar_tensor_tensor(out=a[:, 0:n], in0=pp[:, 1:n + 1], scalar=cwb[:, 3 * j + 1:3 * j + 2],
                                                                 in1=a[:, 0:n], op0=ALU.mult, op1=ALU.add),
                          r=['st2', 'cwb', 'st0'], w=['st0'])
                    P.dve(lambda e, n=n, j=j: e.scalar_tensor_tensor(out=a[:, 0:n], in0=pp[:, 2:n + 2], scalar=cwb[:, 3 * j + 2:3 * j + 3],
                                                                 in1=a[:, 0:n], op0=ALU.mult, op1=ALU.add),
                          r=['st2', 'cwb', 'st0'], w=['st0'])
                    ob = oc[0] % 2
                    P.dve(lambda e, n=n, ob=ob, bg=bg: e.tensor_tensor(out=sbf[ob][:, 0:n], in0=bank(bg)[:, 1:n + 1], in1=a[:, 0:n], op=ALU.mult),
                          r=[f'ps{bg}', 'st0'], w=[f'sbf{ob}'])
                    out_dma(ybT_d[j][:, t0 + s:t0 + s + n], sbf[ob][:, 0:n], f'sbf{ob}')
            for which, off, dstT, sc in ((0, OFF_CQ, cqT_d, 128.0 ** -0.5), (1, OFF_CK, ckT_d, 1.0)):
                for wtile in range(3):
                    wi = nxt_w()
                    load_w(wts[wi], W[:, off + wtile * 512:off + (wtile + 1) * 512], 512, wi)
                    for cg in range(4):
                        hd = wtile * 4 + cg
                        for tt in range(TB // 512):
                            bk = (cg * (TB // 512) + tt) % 4
                            fm_group(wi, cg * 128, 1 + tt * 512, 512, bk)
                            ob = oc[0] % 2
                            P.act(lambda e, bk=bk, ob=ob, sc=sc: e.activation(out=sbf[ob], in_=bank(bk), func=AF.Copy, scale=sc),
                                  r=[f'ps{bk}'], w=[f'sbf{ob}'])
                            out_dma(dstT[hd][:, t0 + tt * 512:t0 + (tt + 1) * 512], sbf[ob], f'sbf{ob}')
            for wtile in range(3):
                wi = nxt_w()
                load_w(wts[wi], W[:, OFF_CV + wtile * 512:OFF_CV + (wtile + 1) * 512], 512, wi)
                for t in range(TB // 128):
                    bk = t % 4
                    for kc in range(16):
                        P.pe(lambda e, kc=kc, t=t, bk=bk, wi=wi: e.matmul(bank(bk), xb[:, kc, 1 + t * 128:1 + (t + 1) * 128], wts[wi][:, kc, :],
                                                                   start=(kc == 0), stop=(kc == 15)),
                             r=[wkey(wi), 'xb'], w=[f'ps{bk}'])
                    ob = oc[0] % 2
                    P.dve(lambda e, bk=bk, ob=ob: e.tensor_copy(sbf[ob], bank(bk)), r=[f'ps{bk}'], w=[f'sbf{ob}'])
                    out_dma(cv_d[t0 + t * 128:t0 + (t + 1) * 128, wtile * 512:(wtile + 1) * 512], sbf[ob], f'sbf{ob}')
        P.barrier()
    mT_d = dram("mT_d", [16, 128, T], BF16)

    def phase_merge(l):
        P.marks.append(('phase_merge', len(P.ops)))
        AR.off = PERSIST
        xb = AR.alloc([16, TB + 2], BF16)
        wsets = [[AR.alloc([16, 256], BF16) for _ in range(4)] + [AR.alloc([4, 256], BF16)] for _ in range(2)]
        ya = AR.alloc([20, 512], BF16)
        sg = [AR.alloc(512, F32) for _ in range(3)]
        t1 = AR.alloc(512, F32)
        t2 = AR.alloc(512, F32)
        mo = [AR.alloc(512, BF16) for _ in range(2)]
        W = w_in[l]
        oc = [0]

        def load_ws(it):
            si = it % 2
            ws = wsets[si]
            c0w = (it % 8) * 256
            for b3 in range(3):
                P.dma(lambda e, b3=b3: e.dma_start(out=ws[b3], in_=W[:, OFF_G + b3 * D + c0w:OFF_G + b3 * D + c0w + 256].rearrange("(c p) n -> p c n", p=128)),
                      w=[f'mw{b3}_{si}'], q='pool')
            P.dma(lambda e: e.dma_start(out=ws[3][:, 0:8, :], in_=w_br_a[l][:, c0w:c0w + 256].rearrange("(c p) n -> p c n", p=128)),
                  w=[f'mw3a_{si}'], q='pool')
            P.dma(lambda e: e.dma_start(out=ws[3][:, 8:16, :], in_=w_br_b[l][:, c0w:c0w + 256].rearrange("(c p) n -> p c n", p=128)),
                  w=[f'mw3b_{si}'], q='pool')
            P.dma(lambda e: e.dma_start(out=ws[4], in_=w_br_c[l][:, c0w:c0w + 256].rearrange("(c p) n -> p c n", p=128)),
                  w=[f'mw4_{si}'], q='pool')

        load_ws(0)
        for it in range(16):
            tb, fcg = it // 8, it % 8
            if fcg == 0:
                load_xb(xb, tb)
            if it + 1 < 16:
                load_ws(it + 1)
            t0 = tb * TB
            si = it % 2
            ws = wsets[si]
            for tt in range(TB // 512):
                c0 = t0 + tt * 512
                P.dma(lambda e, c0=c0: e.dma_start(out=ya[:, 0:8, :], in_=yaT_d[:, :, c0:c0 + 512].rearrange("c p t -> p c t")), w=['ya_a'])
                P.dma(lambda e, c0=c0: e.dma_start(out=ya[:, 8:16, :], in_=ybT_d[:, :, c0:c0 + 512].rearrange("c p t -> p c t")), w=['ya_b'])
                P.dma(lambda e, c0=c0: e.dma_start(out=ya[:, 16:20, :], in_=ycT_d[:, :, c0:c0 + 512].rearrange("c p t -> p c t")), w=['ya_c'])
                for f2 in range(2):
                    fc = fcg * 2 + f2
                    cs = slice(f2 * 128, (f2 + 1) * 128)
                    for b3 in range(3):
                        for kc in range(16):
                            P.pe(lambda e, b3=b3, kc=kc, cs=cs, tt=tt, ws=ws: e.matmul(bank(b3), ws[b3][:, kc, cs], xb[:, kc, 1 + tt * 512:1 + (tt + 1) * 512],
                                                                                      start=(kc == 0), stop=(kc == 15)),
                                 r=[f'mw{b3}_{si}', 'xb'], w=[f'ps{b3}'])
                    for kc in range(8):
                        P.pe(lambda e, kc=kc, cs=cs, ws=ws: e.matmul(bank(3), ws[3][:, kc, cs], ya[:, kc, :], start=(kc == 0), stop=(kc == 7)),
                             r=[f'mw3a_{si}', 'ya_a'], w=['ps3'])
                    for kc in range(8):
                        P.pe(lambda e, kc=kc, cs=cs, ws=ws: e.matmul(bank(4), ws[3][:, 8 + kc, cs], ya[:, 8 + kc, :], start=(kc == 0), stop=(kc == 7)),
                             r=[f'mw3b_{si}', 'ya_b'], w=['ps4'])
                    for kc in range(4):
                        P.pe(lambda e, kc=kc, cs=cs, ws=ws: e.matmul(bank(5), ws[4][:, kc, cs], ya[:, 16 + kc, :], start=(kc == 0), stop=(kc == 3)),
                             r=[f'mw4_{si}', 'ya_c'], w=['ps5'])
                    for b3 in range(3):
                        P.act(lambda e, b3=b3: e.activation(out=sg[b3], in_=bank(b3), func=AF.Sigmoid), r=[f'ps{b3}'], w=[f'sg{b3}'])
                    ob = oc[0] % 2
                    oc[0] += 1
                    P.dve(lambda e: e.tensor_tensor(out=t1, in0=bank(3), in1=sg[0], op=ALU.mult), r=['ps3', 'sg0'], w=['t1'])
                    P.dve(lambda e: e.tensor_tensor(out=t2, in0=bank(4), in1=sg[1], op=ALU.mult), r=['ps4', 'sg1'], w=['t2'])
                    P.dve(lambda e: e.tensor_tensor(out=t1, in0=t1, in1=t2, op=ALU.add), r=['t1', 't2'], w=['t1'])
                    P.dve(lambda e: e.tensor_tensor(out=t2, in0=bank(5), in1=sg[2], op=ALU.mult), r=['ps5', 'sg2'], w=['t2'])
                    P.dve(lambda e, ob=ob: e.tensor_tensor(out=mo[ob], in0=t1, in1=t2, op=ALU.add), r=['t1', 't2'], w=[f'mo{ob}'])
                    P.dma(lambda e, ob=ob, fc=fc, c0=c0: e.dma_start(out=mT_d[fc][:, c0:c0 + 512], in_=mo[ob]),
                          r=[f'mo{ob}'], w=[('mT_d', oc[0])], q='pool')
        P.barrier()

    def phase_wout(l, h_src):
        P.marks.append(('phase_wout', len(P.ops)))
        AR.off = PERSIST
        xb = AR.alloc([16, TB], BF16)
        wts = [AR.alloc([16, 512], BF16) for _ in range(2)]
        hb = [AR.alloc(512, F32) for _ in range(3)]
        oc = [0]
        for tb in range(2):
            t0 = tb * TB
            P.dma(lambda e, t0=t0: e.dma_start(out=xb, in_=mT_d[:, :, t0:t0 + TB].rearrange("c p t -> p c t")), w=['xb'])
            for og in range(4):
                wi = og % 2
                P.dma(lambda e, og=og, wi=wi: e.dma_start(out=wts[wi], in_=w_out[l][:, og * 512:(og + 1) * 512].rearrange("(c p) n -> p c n", p=128)),
                      w=[wkey(wi)], q='pool')
                for t in range(TB // 128):
                    bk = t % 4
                    hi = oc[0] % 3
                    oc[0] += 1
                    r0 = t0 + t * 128
                    P.dma(lambda e, r0=r0, og=og, hi=hi: e.dma_start(out=hb[hi], in_=h_src[r0:r0 + 128, og * 512:(og + 1) * 512]), w=[f'hb{hi}'])
                    for kc in range(16):
                        P.pe(lambda e, kc=kc, t=t, bk=bk, wi=wi: e.matmul(bank(bk), xb[:, kc, t * 128:(t + 1) * 128], wts[wi][:, kc, :],
                                                                          start=(kc == 0), stop=(kc == 15)),
                             r=[wkey(wi), 'xb'], w=[f'ps{bk}'])
                    P.dve(lambda e, bk=bk, hi=hi: e.tensor_tensor(out=hb[hi], in0=bank(bk), in1=hb[hi], op=ALU.add),
                          r=[f'ps{bk}', f'hb{hi}'], w=[f'hb{hi}'])
                    P.dma(lambda e, r0=r0, og=og, hi=hi: e.dma_start(out=h_d[r0:r0 + 128, og * 512:(og + 1) * 512], in_=hb[hi]),
                          r=[f'hb{hi}'], w=[('h_d', oc[0])])
        P.barrier()

    def phase_mlp(l):
        P.marks.append(('phase_mlp', len(P.ops)))
        AR.off = PERSIST
        TM = 512
        xb = AR.alloc([16, TM], BF16)
        hT = AR.alloc([64, TM], BF16)
        wts = [AR.alloc([16, 512], BF16) for _ in range(3)]
        rl = [AR.alloc(512, F32) for _ in range(2)]
        hb = [AR.alloc(512, F32) for _ in range(4)]
        wc = [0]
        oc = [0]
        for tm in range(T // TM):
            t0 = tm * TM
            P.dma(lambda e, t0=t0: e.dma_start(out=xb, in_=xnT_d[:, :, 1 + t0:1 + t0 + TM].rearrange("c p t -> p c t")), w=['xb'])
            for ut in range(16):
                wi = wc[0] % 3
                wc[0] += 1
                P.dma(lambda e, ut=ut, wi=wi: e.dma_start(out=wts[wi], in_=w_up[l][:, ut * 512:(ut + 1) * 512].rearrange("(c p) n -> p c n", p=128)),
                      w=[wkey(wi)], q='pool')
                for cg in range(4):
                    bk = 4 + (ut * 4 + cg) % 4
                    for kc in range(16):
                        P.pe(lambda e, kc=kc, cg=cg, bk=bk, wi=wi: e.matmul(bank(bk), wts[wi][:, kc, cg * 128:(cg + 1) * 128], xb[:, kc, :],
                                                                            start=(kc == 0), stop=(kc == 15)),
                             r=[wkey(wi), 'xb'], w=[f'ps{bk}'])
                    ri = (ut * 4 + cg) % 2
                    P.act(lambda e, bk=bk, ri=ri: e.activation(out=rl[ri], in_=bank(bk), func=AF.Relu), r=[f'ps{bk}'], w=[f'rl{ri}'])
                    P.dve(lambda e, ri=ri, ut=ut, cg=cg: e.tensor_tensor(out=hT[:, ut * 4 + cg, :], in0=rl[ri], in1=rl[ri], op=ALU.mult),
                          r=[f'rl{ri}'], w=[('hT', ut * 4 + cg)])
            for og in range(4):
                for kt in range(4):
                    wi = wc[0] % 3
                    wc[0] += 1
                    P.dma(lambda e, og=og, kt=kt, wi=wi: e.dma_start(out=wts[wi], in_=w_down[l][kt * 2048:(kt + 1) * 2048, og * 512:(og + 1) * 512].rearrange("(c p) n -> p c n", p=128)),
                          w=[wkey(wi)], q='pool')
                    for t in range(4):
                        for kc in range(16):
                            P.pe(lambda e, kc=kc, kt=kt, t=t, wi=wi: e.matmul(bank(t), hT[:, kt * 16 + kc, t * 128:(t + 1) * 128], wts[wi][:, kc, :],
                                                                              start=(kt == 0 and kc == 0), stop=(kt == 3 and kc == 15)),
                                 r=[wkey(wi), ('hT', kt * 16 + kc)], w=[f'ps{t}'])
                for t in range(4):
                    hi = oc[0] % 4
                    oc[0] += 1
                    r0 = t0 + t * 128
                    P.dma(lambda e, r0=r0, og=og, hi=hi: e.dma_start(out=hb[hi], in_=h_d[r0:r0 + 128, og * 512:(og + 1) * 512]),
                          r=[('h_d2', r0, og)], w=[f'hb{hi}'])
                    P.dve(lambda e, t=t, hi=hi: e.tensor_tensor(out=hb[hi], in0=bank(t), in1=hb[hi], op=ALU.add),
                          r=[f'ps{t}', f'hb{hi}'], w=[f'hb{hi}'])
                    P.dma(lambda e, r0=r0, og=og, hi=hi: e.dma_start(out=h_d[r0:r0 + 128, og * 512:(og + 1) * 512], in_=hb[hi]),
                          r=[f'hb{hi}'], w=[('h_d2', r0, og)])
        P.barrier()
    oext_d = dram("oext_d", [3, T, 516], F32)
    rv_d = dram("rv_d", [3, 12, 768], F32)
    AR.off = PERSIST
    BT = AR.alloc([36, 128], F32)
    amask = AR.alloc([3, 128], F32)
    jmat = AR.alloc(128, F32)
    P.dma(lambda e: e.dma_start(out=amask, in_=amask_in), w=['amask'])
    P.dma(lambda e: e.dma_start(out=jmat, in_=jmat_in), w=['jmat'])
    PERSIST = AR.off

    def setup_bias():
        relb = AR.alloc(12, F32)
        eoh = AR.alloc([3, 768], F32)
        rvs = AR.alloc(768, F32)
        hs = [AR.alloc(128, F32) for _ in range(2)]
        P.dma(lambda e: e.dma_start(out=relb[0:32, :], in_=relb_in), w=['relb'])
        P.dma(lambda e: e.dma_start(out=eoh[0:32], in_=eoh_in), w=['eoh'])
        for g in range(3):
            for half in range(2):
                P.pe(lambda e, g=g, half=half: e.matmul(bank(half)[0:12, 0:384], relb[0:32, :], eoh[0:32, g, half * 384:(half + 1) * 384], start=True, stop=True),
                     r=['relb', 'eoh'], w=[f'ps{half}'])
                P.dve(lambda e, half=half: e.tensor_copy(rvs[0:12, half * 384:(half + 1) * 384], bank(half)[0:12, 0:384]), r=[f'ps{half}'], w=['rvs'])
            P.dma(lambda e, g=g: e.dma_start(out=rv_d[g], in_=rvs[0:12, :]), r=['rvs'], w=[('rv_d', g)])
        n = 0
        for g in range(3):
            for j in range(4):
                for dl in range(3):
                    hi = n % 2
                    src = rv_d[g, g * 4 + j, dl * 256:dl * 256 + 128]
                    hap = bass.AP(tensor=src.tensor, offset=src.offset, ap=[[1, 128], [1, 128]])
                    P.dma(lambda e, hap=hap, hi=hi: e.dma_start(out=hs[hi], in_=hap), r=[('rv_d', g)], w=[f'hs{hi}'])
                    bk = 2 + n % 2
                    P.pe(lambda e, hi=hi, bk=bk: e.matmul(bank(bk)[:, 0:128], hs[hi], jmat, start=True, stop=True), r=[f'hs{hi}', 'jmat'], w=[f'ps{bk}'])
                    P.dve(lambda e, bk=bk, idx=(g * 4 + j) * 3 + dl, dl=dl: e.tensor_tensor(out=BT[:, idx, :], in0=bank(bk)[:, 0:128], in1=amask[:, dl, :], op=ALU.add),
                          r=[f'ps{bk}', 'amask'], w=['BT'])
                    n += 1
        P.barrier()

    setup_bias()

    def phase_attn():
        P.marks.append(('phase_attn', len(P.ops)))
        AR.off = PERSIST
        QT4 = AR.alloc([4, T], BF16)
        KT4 = AR.alloc([4, T], BF16)
        vt = AR.alloc([NT, 4, 130], BF16)
        sT = [AR.alloc([3, 128], F32) for _ in range(4)]
        pT = [AR.alloc([3, 128], BF16) for _ in range(4)]
        ucnt = [0]
        ob = [AR.alloc([4, 129], F32) for _ in range(2)]
        P.dve(lambda e: e.memset(vt[:, :, :, 128:129], 1.0), w=['vt_ones'])
        n = 0
        for g, d in enumerate((1, 4, 16)):
            Ls = T // d
            NTs = Ls // 128
            P.dma(lambda e, g=g: e.dma_start(out=QT4, in_=cqT_d[g * 4:(g + 1) * 4].rearrange("h p t -> p h t")), w=['QT4'])
            P.dma(lambda e, g=g: e.dma_start(out=KT4, in_=ckT_d[g * 4:(g + 1) * 4].rearrange("h p t -> p h t")), w=['KT4'])
            cvr = cv_d.rearrange("(n dd) c -> dd n c", dd=d)
            oxr = oext_d[g].rearrange("(n dd) c -> dd n c", dd=d)
            for r in range(d):
                for kt in range(NTs):
                    P.dma(lambda e, r=r, g=g, kt=kt, cvr=cvr: e.dma_start(
                        out=vt[:, kt, :, 0:128],
                        in_=cvr[r][kt * 128:(kt + 1) * 128, g * 512:(g + 1) * 512].rearrange("i (j c) -> i j c", j=4)),
                        r=['vt_ones'], w=[('vt', kt)])
                units = [(qt, jp) for qt in range(NTs) for jp in range(2)]

                def stage_s(ui, qt, jp, g=g, d=d, r=r, NTs=NTs):
                    up = ui % 2
                    dls = [dl for dl in range(3) if 0 <= qt + dl - 1 < NTs]
                    d0, d1 = dls[0], dls[-1] + 1
                    q0 = r + d * qt * 128
                    qs = slice(q0, q0 + d * 127 + 1, d)
                    for jj in range(2):
                        j = 2 * jp + jj
                        bk = 2 * up + jj
                        sb_i = 2 * up + jj
                        for dl in dls:
                            k0 = r + d * (qt + dl - 1) * 128
                            ks = slice(k0, k0 + d * 127 + 1, d)
                            P.pe(lambda e, bk=bk, dl=dl, j=j, ks=ks: e.matmul(bank(bk)[:, dl * 128:(dl + 1) * 128], KT4[:, j, ks], QT4[:, j, qs], start=True, stop=True),
                                 r=['KT4', 'QT4'], w=[f'ps{bk}'])
                    for jj in range(2):
                        j = 2 * jp + jj
                        bk = 2 * up + jj
                        sb_i = 2 * up + jj
                        bidx = (g * 4 + j) * 3
                        P.dve(lambda e, bk=bk, bidx=bidx, sb_i=sb_i: e.tensor_tensor(
                            out=sT[sb_i][:, d0:d1, :], in0=bank(bk)[:, d0 * 128:d1 * 128].rearrange("p (a b) -> p a b", b=128),
                            in1=BT[:, bidx + d0:bidx + d1, :], op=ALU.add),
                            r=[f'ps{bk}', 'BT'], w=[f'sT{sb_i}'])
                        P.act(lambda e, sb_i=sb_i: e.activation(out=pT[sb_i][:, d0:d1, :], in_=sT[sb_i][:, d0:d1, :], func=AF.Exp),
                              r=[f'sT{sb_i}'], w=[f'pT{sb_i}'])
                        cross = None
                        if qt == NTs // 2 - 1:
                            cross = 2
                        elif qt == NTs // 2:
                            cross = 0
                        if cross is not None and cross in dls:
                            P.dve(lambda e, cross=cross, sb_i=sb_i: e.tensor_scalar(pT[sb_i][:, cross, :], pT[sb_i][:, cross, :], keepc, None, ALU.mult),
                                  r=[f'pT{sb_i}', 'keepc'], w=[f'pT{sb_i}'])

                def stage_pv(ui, qt, jp, g=g, d=d, r=r, NTs=NTs, oxr=oxr):
                    up = ui % 2
                    o2 = qt % 2
                    dls = [dl for dl in range(3) if 0 <= qt + dl - 1 < NTs]
                    obk = 4 + up
                    for jj in range(2):
                        j = 2 * jp + jj
                        sb_i = 2 * up + jj
                        oc0 = jj * 129
                        for dl in dls:
                            kt = qt + dl - 1
                            P.pe(lambda e, oc0=oc0, dl=dl, kt=kt, j=j, sb_i=sb_i, first=(dl == dls[0]), last=(dl == dls[-1]):
                                 e.matmul(bank(obk)[:, oc0:oc0 + 129], pT[sb_i][:, dl, :], vt[:, kt, j, 0:129], start=first, stop=last),
                                 r=[f'pT{sb_i}', ('vt', kt), 'vt_ones'], w=[f'ps{obk}'])
                    P.act(lambda e: e.copy(ob[o2][:, jp * 2:jp * 2 + 2, :], bank(obk)[:, 0:258].rearrange("p (a b) -> p a b", b=129)),
                          r=[f'ps{obk}'], w=[f'ob{o2}_{jp}'])
                    if jp == 1:
                        P.dma(lambda e: e.dma_start(out=oxr[r][qt * 128:(qt + 1) * 128, :], in_=ob[o2].rearrange("p a b -> p (a b)")),
                              r=[f'ob{o2}_0', f'ob{o2}_1'], w=[('oext', g, r, qt)], q='pool')

                stage_s(ucnt[0], *units[0])
                for ui, (qt, jp) in enumerate(units):
                    if ui + 1 < len(units):
                        stage_s(ucnt[0] + ui + 1, *units[ui + 1])
                    stage_pv(ucnt[0] + ui, qt, jp)
                ucnt[0] += len(units)
        P.barrier()
        AR.off = PERSIST
        og = [[AR.alloc([4, 129], F32) for _ in range(3)] for _ in range(2)]
        rec = [AR.alloc(4, F32) for _ in range(2)]
        yb = [AR.alloc([4, 128], BF16) for _ in range(2)]
        ytb = [AR.alloc([4, 128], BF16) for _ in range(2)]
        for t in range(NT):
            b = t % 2
            for g in range(3):
                P.dma(lambda e, g=g, t=t, b=b: e.dma_start(out=og[b][g].rearrange("p a b -> p (a b)"), in_=oext_d[g][t * 128:(t + 1) * 128, :]), w=[f'og{b}{g}'])
            P.dve(lambda e, b=b: e.tensor_tensor(out=og[b][0], in0=og[b][0], in1=og[b][1], op=ALU.add), r=[f'og{b}0', f'og{b}1'], w=[f'og{b}0'])
            P.dve(lambda e, b=b: e.tensor_tensor(out=og[b][0], in0=og[b][0], in1=og[b][2], op=ALU.add), r=[f'og{b}0', f'og{b}2'], w=[f'og{b}0'])
            P.dve(lambda e, b=b: e.reciprocal(rec[b], og[b][0][:, :, 128]), r=[f'og{b}0'], w=[f'rec{b}'])
            P.dve(lambda e, b=b: e.tensor_tensor(out=yb[b], in0=og[b][0][:, :, 0:128], in1=rec[b].unsqueeze(2).to_broadcast([128, 4, 128]), op=ALU.mult),
                  r=[f'og{b}0', f'rec{b}'], w=[f'yb{b}'])
            pb = bank(6 + b, BF16).rearrange("p (a b) -> p a b", b=128)
            for j in range(4):
                P.pe(lambda e, b=b, j=j, pb=pb: e.transpose(pb[:, j, :], yb[b][:, j, :], ident_b), r=[f'yb{b}', 'ident_b'], w=[f'ps{6 + b}'])
            P.act(lambda e, b=b, pb=pb: e.copy(ytb[b], pb[:, 0:4, :]), r=[f'ps{6 + b}'], w=[f'ytb{b}'])
            P.dma(lambda e, t=t, b=b: e.dma_start(out=ycT_d[:, :, t * 128:(t + 1) * 128].rearrange("c p t -> p c t"), in_=ytb[b]),
                  r=[f'ytb{b}'], w=[('ycT_d', t)], q='pool')
        P.barrier()
    of_d = dram("of_d", [2, 8, T, 128], F32)

    def phase_gdn(l):
        P.marks.append(('phase_gdn', len(P.ops)))
        AR.off = PERSIST
        ab = AR.alloc([NT, 32], F32)
        gp = AR.alloc(32, F32)
        negA = AR.alloc(16, F32)
        gg = AR.alloc([NT, 16], F32)
        bt = AR.alloc([NT, 16], F32)
        nbt = AR.alloc([NT, 16], F32)
        gc = AR.alloc([NT, 16], F32)
        gtot = AR.alloc([NT, 16], F32)
        egt = AR.alloc([NT, 16], F32)
        kds = AR.alloc([NT, 16], F32)
        bw = AR.alloc([NT, 16], F32)
        S32 = AR.alloc([16, 128], F32)
        Sbf = AR.alloc([16, 2, 128], BF16)
        tril_i, tril_s, triu_i, triu_s = cst[:, 1, :], cst[:, 2, :], cst[:, 3, :], cst[:, 4, :]
        P.dma(lambda e: e.dma_start(out=ab, in_=ab_d.rearrange("(t p) c -> p t c", p=128)), w=['ab'])
        P.dma(lambda e: e.dma_start(out=gp, in_=gdnp_in[l:l + 1, :].partition_broadcast(128)), w=['gp'])
        P.act(lambda e: e.activation(out=negA, in_=gp[:, 0:16], func=AF.Exp), r=['gp'], w=['negA'])
        P.dve(lambda e: e.tensor_scalar(negA, negA, -1.0, None, ALU.mult), r=['negA'], w=['negA'])
        P.dve(lambda e: e.tensor_tensor(out=gg, in0=ab[:, :, 0:16], in1=gp[:, 16:32].unsqueeze(1).to_broadcast([128, NT, 16]), op=ALU.add),
              r=['ab', 'gp'], w=['gg'])
        P.act(lambda e: e.activation(out=gg, in_=gg, func=AF.Exp), r=['gg'], w=['gg'])
        P.act(lambda e: e.activation(out=gg, in_=gg, func=AF.Ln, bias=1.0), r=['gg'], w=['gg'])
        P.dve(lambda e: e.tensor_tensor(out=gg, in0=gg, in1=negA.unsqueeze(1).to_broadcast([128, NT, 16]), op=ALU.mult),
              r=['gg', 'negA'], w=['gg'])
        P.act(lambda e: e.activation(out=bt, in_=ab[:, :, 16:32], func=AF.Sigmoid), r=['ab'], w=['bt'])
        P.dve(lambda e: e.tensor_scalar(nbt, bt, -1.0, None, ALU.mult), r=['bt'], w=['nbt'])
        for t in range(NT):
            P.pe(lambda e, t=t: e.matmul(bank(0)[:, t * 16:t * 16 + 8], triu_i, gg[:, t, 0:8], start=True, stop=True), r=['gg', 'cst'], w=['ps0'])
            P.pe(lambda e, t=t: e.matmul(bank(0)[:, t * 16 + 8:t * 16 + 16], tril_i, gg[:, t, 8:16], start=True, stop=True), r=['gg', 'cst'], w=['ps0'])
            P.pe(lambda e, t=t: e.matmul(bank(1)[:, t * 16:t * 16 + 16], ones_f, gg[:, t, :], start=True, stop=True), r=['gg', 'ones_f'], w=['ps1'])
        P.dve(lambda e: e.tensor_copy(gc.rearrange("p a b -> p (a b)"), bank(0)[:, 0:NT * 16]), r=['ps0'], w=['gc'])
        P.dve(lambda e: e.tensor_copy(gtot.rearrange("p a b -> p (a b)"), bank(1)[:, 0:NT * 16]), r=['ps1'], w=['gtot'])
        P.act(lambda e: e.activation(out=egt, in_=gtot, func=AF.Exp), r=['gtot'], w=['egt'])
        P.dve(lambda e: e.tensor_tensor(out=kds, in0=gtot, in1=gc, op=ALU.subtract), r=['gtot', 'gc'], w=['kds'])
        P.act(lambda e: e.activation(out=kds, in_=kds, func=AF.Exp), r=['kds'], w=['kds'])
        P.act(lambda e: e.activation(out=bw, in_=gc, func=AF.Exp), r=['gc'], w=['bw'])
        P.dve(lambda e: e.tensor_tensor(out=bw, in0=bw, in1=bt, op=ALU.mult), r=['bw', 'bt'], w=['bw'])
        P.dve(lambda e: e.memset(S32, 0.0), w=['S32'])
        P.dve(lambda e: e.memset(Sbf, 0.0), w=['Sbf'])
        NS = 8
        slots = []
        for s in range(NS):
            d_ = dict(
                qkv=AR.alloc([3, 128], BF16), gU=AR.alloc(128, F32), Em=AR.alloc(128, F32), E=AR.alloc(128, F32),
                egrow=AR.alloc(128, F32), ES=AR.alloc(128, F32), EI=AR.alloc(128, F32),
                N=[AR.alloc([2, 128], BF16) for _ in range(2)], intra=AR.alloc(128, BF16), intraT=AR.alloc(128, BF16),
                qg=AR.alloc(128, BF16), kdec=AR.alloc(128, BF16), X32=AR.alloc(256, F32), Xb=AR.alloc(256, BF16), wlo=AR.alloc(128, BF16), Pm=[AR.alloc([2, 128], BF16) for _ in range(2)], Cn=AR.alloc([2, 128], BF16), M1=AR.alloc([2, 128], BF16),
                wT=AR.alloc([2, 128], BF16), vnew=AR.alloc(128, BF16), osb=AR.alloc(128, F32))
            slots.append(d_)

        def chain_step(c, h, dr, s):
            B = slots[s]
            K = lambda nm: f'g{s}_{nm}'
            b0, b1, b2, b3 = s, s, s, s
            pb0, pb2, pb3 = bank(b0), bank(b2), bank(b3)
            pb1 = bank(b1, BF16)
            j16 = dr * 8 + h
            Ud = triu_i if dr == 0 else tril_i
            mS = tril_s if dr == 0 else triu_s
            mI = tril_i if dr == 0 else triu_i
            cs = slice(c * 128, (c + 1) * 128)
            qc, kc_, vc = B['qkv'][:, 0, :], B['qkv'][:, 1, :], B['qkv'][:, 2, :]
            for i3, src in enumerate((qT_d, kT_d, vT_d)):
                P.dma(lambda e, i3=i3, src=src: e.dma_start(out=B['qkv'][:, i3, :], in_=src[h][:, cs]), w=[K(f'qkv{i3}')])
            qk = [K('qkv0'), K('qkv1'), K('qkv2')]
            P.pe(lambda e: e.matmul(pb0[:, 0:128], kc_, kc_, start=True, stop=True), r=[qk[1]], w=[f'ps{b0}'])
            P.pe(lambda e: e.matmul(pb0[:, 128:256], qc, kc_, start=True, stop=True), r=[qk[0], qk[1]], w=[f'ps{b0}'])
            P.dve(lambda e: e.tensor_scalar(B['gU'], Ud, gg[:, c, j16:j16 + 1], None, ALU.mult), r=['gg', 'cst'], w=[K('gU')])
            P.pe(lambda e: e.matmul(pb0[:, 256:384], ones_f, B['gU'], start=True, stop=True), r=[K('gU'), 'ones_f'], w=[f'ps{b0}'])
            P.pe(lambda e: e.transpose(pb1[:, 768:896], kc_, ident_b), r=[qk[1], 'ident_b'], w=[f'ps{b1}'])
            P.pe(lambda e: e.transpose(pb1[:, 896:1024], vc, ident_b), r=[qk[2], 'ident_b'], w=[f'ps{b1}'])
            yield
            P.dve(lambda e: e.tensor_scalar(B['Em'], pb0[:, 256:384], gc[:, c, j16:j16 + 1], 0.0, ALU.subtract, ALU.max),
                  r=[f'ps{b0}', 'gc'], w=[K('Em')])
            P.act(lambda e: e.activation(out=B['E'], in_=B['Em'], func=AF.Exp, scale=-1.0), r=[K('Em')], w=[K('E')])
            P.act(lambda e: e.activation(out=B['egrow'], in_=pb0[:, 256:384], func=AF.Exp), r=[f'ps{b0}'], w=[K('egrow')])
            P.dve(lambda e: e.tensor_tensor(out=B['ES'], in0=B['E'], in1=mS, op=ALU.mult), r=[K('E'), 'cst'], w=[K('ES')])
            P.dve(lambda e: e.tensor_tensor(out=B['EI'], in0=B['E'], in1=mI, op=ALU.mult), r=[K('E'), 'cst'], w=[K('EI')])
            yield
            N0 = B['N'][0]
            P.dve(lambda e: e.scalar_tensor_tensor(out=N0[:, 0, :], in0=pb0[:, 0:128], scalar=nbt[:, c, j16:j16 + 1], in1=B['ES'], op0=ALU.mult, op1=ALU.mult),
                  r=[f'ps{b0}', 'nbt', K('ES')], w=[K('N0a')])
            P.dve(lambda e: e.tensor_tensor(out=B['intra'], in0=pb0[:, 128:256], in1=B['EI'], op=ALU.mult), r=[f'ps{b0}', K('EI')], w=[K('intra')])
            P.dve(lambda e: e.tensor_tensor(out=B['qg'], in0=qc, in1=B['egrow'], op=ALU.mult), r=[qk[0], K('egrow')], w=[K('qg')])
            X32, Xb = B['X32'], B['Xb']
            P.act(lambda e: e.activation(out=B['kdec'], in_=pb1[:, 768:896], func=AF.Copy, scale=kds[:, c, j16:j16 + 1]), r=[f'ps{b1}', 'kds'], w=[K('kdec')])
            P.act(lambda e: e.activation(out=X32[:, 128:256], in_=pb1[:, 768:896], func=AF.Copy, scale=bw[:, c, j16:j16 + 1]), r=[f'ps{b1}', 'bw'], w=[K('X32')])
            P.act(lambda e: e.activation(out=X32[:, 0:128], in_=pb1[:, 896:1024], func=AF.Copy, scale=bt[:, c, j16:j16 + 1]), r=[f'ps{b1}', 'bt'], w=[K('X32')])
            P.act(lambda e: e.copy(Xb, X32), r=[K('X32')], w=[K('Xb')])
            yield
            P.pe(lambda e: e.transpose(pb1[:, 256:384], N0[:, 0, :], ident_b), r=[K('N0a'), 'ident_b'], w=[f'ps{b1}'])
            P.pe(lambda e: e.transpose(pb1[:, 384:512], B['intra'], ident_b), r=[K('intra'), 'ident_b'], w=[f'ps{b1}'])
            P.act(lambda e: e.copy(N0[:, 1, :], pb1[:, 256:384]), r=[f'ps{b1}'], w=[K('N0b')])
            P.act(lambda e: e.copy(B['intraT'], pb1[:, 384:512]), r=[f'ps{b1}'], w=[K('intraT')])
            yield
            bd16 = cst[:, 5, :]
            N0 = B['N'][0]
            Nb = B['N'][1]
            P.dve(lambda e: e.tensor_tensor(out=Nb[:, 0, :], in0=N0[:, 0, :], in1=bd16, op=ALU.mult), r=[K('N0a'), 'cst'], w=[K('N1a')])
            P.dve(lambda e: e.tensor_tensor(out=Nb[:, 1, :], in0=N0[:, 1, :], in1=bd16, op=ALU.mult), r=[K('N0b'), 'cst'], w=[K('N1b')])
            Pc = B['Pm'][0]
            P.dve(lambda e: e.tensor_tensor(out=Pc[:, 0, :], in0=Nb[:, 0, :], in1=ident_f, op=ALU.add), r=[K('N1a'), 'cst'], w=[K('P0')])
            P.dve(lambda e: e.tensor_tensor(out=Pc[:, 1, :], in0=Nb[:, 1, :], in1=ident_f, op=ALU.add), r=[K('N1b'), 'cst'], w=[K('P0')])
            yield
            pi = 0
            Ncur, Ncur_k = Nb, [K('N1a'), K('N1b')]
            scr = [B['M1'], B['Cn']]
            for lev in range(3):
                Nn = scr[lev % 2]
                nk = [K(f'scr{lev % 2}')]
                P.pe(lambda e, Ncur=Ncur: e.matmul(pb2[:, 0:128], Ncur[:, 1, :], Ncur[:, 0, :], start=True, stop=True), r=Ncur_k, w=[f'ps{b2}'])
                P.pe(lambda e, Ncur=Ncur: e.matmul(pb2[:, 128:256], Ncur[:, 0, :], Ncur[:, 1, :], start=True, stop=True), r=Ncur_k, w=[f'ps{b2}'])
                yield
                P.dve(lambda e, Nn=Nn: e.tensor_copy(Nn.rearrange("p a b -> p (a b)"), pb2[:, 0:256]), r=[f'ps{b2}'], w=nk)
                yield
                Pc = B['Pm'][pi]
                Pn = B['Pm'][1 - pi]
                pk, pnk = [K(f'P{pi}')], [K(f'P{1 - pi}')]
                P.pe(lambda e, Pc=Pc: e.matmul(pb2[:, 256:384], ident_b, Pc[:, 0, :], start=True, stop=False), r=pk + ['ident_b'], w=[f'ps{b2}'])
                P.pe(lambda e, Pc=Pc, Nn=Nn: e.matmul(pb2[:, 256:384], Pc[:, 1, :], Nn[:, 0, :], start=False, stop=True), r=pk + nk, w=[f'ps{b2}'])
                P.pe(lambda e, Pc=Pc: e.matmul(pb2[:, 384:512], ident_b, Pc[:, 1, :], start=True, stop=False), r=pk + ['ident_b'], w=[f'ps{b2}'])
                P.pe(lambda e, Pc=Pc, Nn=Nn: e.matmul(pb2[:, 384:512], Nn[:, 0, :], Pc[:, 1, :], start=False, stop=True), r=pk + nk, w=[f'ps{b2}'])
                yield
                P.act(lambda e, Pn=Pn: e.copy(Pn.rearrange("p a b -> p (a b)"), pb2[:, 256:512]), r=[f'ps{b2}'], w=pnk)
                yield
                pi = 1 - pi
                Ncur, Ncur_k = Nn, nk
            for mi in range(3):
                cm = cst[:, 6 + mi, :]
                Dc = B['Pm'][pi]
                Dn = B['Pm'][1 - pi]
                dk, dnk = [K(f'P{pi}')], [K(f'P{1 - pi}')]
                Cn, M1 = B['Cn'], B['M1']
                P.dve(lambda e, cm=cm: e.tensor_tensor(out=Cn[:, 0, :], in0=N0[:, 0, :], in1=cm, op=ALU.mult), r=[K('N0a'), 'cst'], w=[K('scr1')])
                P.dve(lambda e, cm=cm: e.tensor_tensor(out=Cn[:, 1, :], in0=N0[:, 1, :], in1=cm, op=ALU.mult), r=[K('N0b'), 'cst'], w=[K('scr1')])
                yield
                P.pe(lambda e, Dc=Dc: e.matmul(pb2[:, 0:128], Cn[:, 1, :], Dc[:, 0, :], start=True, stop=True), r=[K('scr1')] + dk, w=[f'ps{b2}'])
                P.pe(lambda e, Dc=Dc: e.matmul(pb2[:, 128:256], Cn[:, 0, :], Dc[:, 1, :], start=True, stop=True), r=[K('scr1')] + dk, w=[f'ps{b2}'])
                yield
                P.act(lambda e: e.copy(M1.rearrange("p a b -> p (a b)"), pb2[:, 0:256]), r=[f'ps{b2}'], w=[K('scr0')])
                yield
                P.pe(lambda e, Dc=Dc: e.matmul(pb2[:, 256:384], ident_b, Dc[:, 0, :], start=True, stop=False), r=dk + ['ident_b'], w=[f'ps{b2}'])
                P.pe(lambda e, Dc=Dc: e.matmul(pb2[:, 256:384], Dc[:, 1, :], M1[:, 0, :], start=False, stop=True), r=dk + [K('scr0')], w=[f'ps{b2}'])
                P.pe(lambda e, Dc=Dc: e.matmul(pb2[:, 384:512], ident_b, Dc[:, 1, :], start=True, stop=False), r=dk + ['ident_b'], w=[f'ps{b2}'])
                P.pe(lambda e, Dc=Dc: e.matmul(pb2[:, 384:512], Dc[:, 0, :], M1[:, 1, :], start=False, stop=True), r=dk + [K('scr0')], w=[f'ps{b2}'])
                yield
                P.act(lambda e, Dn=Dn: e.copy(Dn.rearrange("p a b -> p (a b)"), pb2[:, 256:512]), r=[f'ps{b2}'], w=dnk)
                yield
                pi = 1 - pi
            Df = B['Pm'][pi]
            P.pe(lambda e, Df=Df: e.matmul(pb2[:, 256:512], Df[:, 1, :], Xb, start=True, stop=True), r=[K(f'P{pi}'), K('Xb')], w=[f'ps{b2}'])
            yield
            P.dve(lambda e: e.tensor_copy(X32, pb2[:, 256:512]), r=[f'ps{b2}'], w=[K('X32')])
            P.act(lambda e: e.copy(Xb, X32), r=[K('X32')], w=[K('Xb')])
            P.dve(lambda e: e.tensor_tensor(out=B['wlo'], in0=X32[:, 128:256], in1=Xb[:, 128:256], op=ALU.subtract), r=[K('X32'), K('Xb')], w=[K('wlo')])
            yield
            P.pe(lambda e: e.transpose(pb1[:, 0:128], Xb[:, 128:256], ident_b), r=[K('Xb'), 'ident_b'], w=[f'ps{b1}'])
            P.pe(lambda e: e.transpose(pb1[:, 128:256], B['wlo'], ident_b), r=[K('wlo'), 'ident_b'], w=[f'ps{b1}'])
            yield
            P.act(lambda e: e.copy(B['wT'].rearrange("p a b -> p (a b)"), pb1[:, 0:256]), r=[f'ps{b1}'], w=[K('wT')])
            yield
            Sb = Sbf[:, j16, 0, :]
            Sl = Sbf[:, j16, 1, :]
            S3 = S32[:, j16, :]
            sk, s3k = ('Sbf', j16), ('S32', j16)
            P.pe(lambda e: e.matmul(pb3[:, 0:128], B['wT'][:, 0, :], Sb, start=True, stop=False), r=[K('wT'), sk], w=[f'ps{b3}'])
            P.pe(lambda e: e.matmul(pb3[:, 0:128], B['wT'][:, 1, :], Sb, start=False, stop=False), r=[K('wT'), sk], w=[f'ps{b3}'])
            P.pe(lambda e: e.matmul(pb3[:, 0:128], B['wT'][:, 0, :], Sl, start=False, stop=True), r=[K('wT'), sk], w=[f'ps{b3}'])
            yield
            P.dve(lambda e: e.tensor_tensor(out=B['vnew'], in0=X32[:, 0:128], in1=pb3[:, 0:128], op=ALU.subtract), r=[K('X32'), f'ps{b3}'], w=[K('vnew')])
            yield
            P.pe(lambda e: e.matmul(pb3[:, 128:256], B['qg'], Sb, start=True, stop=False), r=[K('qg'), sk], w=[f'ps{b3}'])
            P.pe(lambda e: e.matmul(pb3[:, 128:256], B['intraT'], B['vnew'], start=False, stop=True), r=[K('intraT'), K('vnew')], w=[f'ps{b3}'])
            P.pe(lambda e: e.matmul(pb3[:, 256:384], B['kdec'], B['vnew'], start=True, stop=True), r=[K('kdec'), K('vnew')], w=[f'ps{b3}'])
            yield
            P.act(lambda e: e.copy(B['osb'], pb3[:, 128:256]), r=[f'ps{b3}'], w=[K('osb')])
            P.dma(lambda e: e.dma_start(out=of_d[dr, h, cs, :], in_=B['osb']), r=[K('osb')], w=[('of_d', dr, h, c)], q='pool')
            P.dve(lambda e: e.scalar_tensor_tensor(out=S3, in0=S3, scalar=egt[:, c, j16:j16 + 1], in1=pb3[:, 256:384], op0=ALU.mult, op1=ALU.add),
                  r=[s3k, 'egt', f'ps{b3}'], w=[s3k])
            if (dr == 0 and c == NT // 2 - 1) or (dr == 1 and c == NT // 2):
                P.dve(lambda e: e.tensor_scalar(S3, S3, keepc, None, ALU.mult), r=[s3k, 'keepc'], w=[s3k])
            P.act(lambda e: e.copy(Sb, S3), r=[s3k], w=[sk])
            P.dve(lambda e: e.tensor_tensor(out=Sl, in0=S3, in1=Sb, op=ALU.subtract), r=[s3k, sk], w=[sk])

        todo = []
        for i in range(NT):
            for h in range(8):
                todo.append((i, h, 0))
                todo.append((NT - 1 - i, h, 1))
        todo.reverse()
        active = [None] * NS
        while todo or any(a is not None for a in active):
            for s_ in range(NS):
                if active[s_] is None and todo:
                    active[s_] = chain_step(*todo.pop(), s_)
                if active[s_] is not None:
                    try:
                        next(active[s_])
                    except StopIteration:
                        active[s_] = None
        P.barrier()
        AR.off = PERSIST
        hnb = AR.alloc(128, F32)
        P.dma(lambda e: e.dma_start(out=hnb, in_=hn_in[l:l + 1, :].partition_broadcast(128)), w=['hnb'])
        o1 = [AR.alloc([8, 128], F32) for _ in range(2)]
        o2 = [AR.alloc([8, 128], F32) for _ in range(2)]
        zz = [AR.alloc([8, 128], F32) for _ in range(2)]
        sqj = AR.alloc([8, 128], F32)
        ss = [AR.alloc(8, F32) for _ in range(2)]
        yb = [AR.alloc([8, 128], BF16) for _ in range(2)]
        ytb = [AR.alloc([8, 128], BF16) for _ in range(2)]
        for t in range(NT):
            b = t % 2
            ts_ = slice(t * 128, (t + 1) * 128)
            P.dma(lambda e, b=b, ts_=ts_: e.dma_start(out=o1[b], in_=of_d[0][:, ts_, :].rearrange("h t d -> t h d")), w=[f'o1{b}'])
            P.dma(lambda e, b=b, ts_=ts_: e.dma_start(out=o2[b], in_=of_d[1][:, ts_, :].rearrange("h t d -> t h d")), w=[f'o2{b}'])
            P.dma(lambda e, b=b, ts_=ts_: e.dma_start(out=zz[b].rearrange("p a b -> p (a b)"), in_=z_d[ts_, :]), w=[f'zz{b}'])
            P.dve(lambda e, b=b: e.tensor_tensor(out=o1[b], in0=o1[b], in1=o2[b], op=ALU.add), r=[f'o1{b}', f'o2{b}'], w=[f'o1{b}'])
            P.dve(lambda e, b=b: e.tensor_tensor(out=sqj, in0=o1[b], in1=o1[b], op=ALU.mult), r=[f'o1{b}'], w=['sqj'])
            P.dve(lambda e, b=b: e.tensor_reduce(out=ss[b], in_=sqj, axis=AX.X, op=ALU.add), r=['sqj'], w=[f'ss{b}'])
            P.act(lambda e, b=b: e.activation(out=ss[b], in_=ss[b], func=AF.Sqrt, scale=1.0 / 128, bias=epsc), r=[f'ss{b}', 'epsc'], w=[f'ss{b}'])
            P.dve(lambda e, b=b: e.reciprocal(ss[b], ss[b]), r=[f'ss{b}'], w=[f'ss{b}'])
            P.dve(lambda e, b=b: e.tensor_tensor(out=o1[b], in0=o1[b], in1=ss[b].unsqueeze(2).to_broadcast([128, 8, 128]), op=ALU.mult),
                  r=[f'o1{b}', f'ss{b}'], w=[f'o1{b}'])
            P.dve(lambda e, b=b: e.tensor_tensor(out=o1[b], in0=o1[b], in1=hnb.unsqueeze(1).to_broadcast([128, 8, 128]), op=ALU.mult),
                  r=[f'o1{b}', 'hnb'], w=[f'o1{b}'])
            P.dve(lambda e, b=b: e.tensor_tensor(out=yb[b], in0=o1[b], in1=zz[b], op=ALU.mult), r=[f'o1{b}', f'zz{b}'], w=[f'yb{b}'])
            pb = bank(6 + b, BF16).rearrange("p (a b) -> p a b", b=128)
            for j in range(8):
                P.pe(lambda e, b=b, j=j, pb=pb: e.transpose(pb[:, j, :], yb[b][:, j, :], ident_b), r=[f'yb{b}', 'ident_b'], w=[f'ps{6 + b}'])
            P.act(lambda e, b=b, pb=pb: e.copy(ytb[b], pb), r=[f'ps{6 + b}'], w=[f'ytb{b}'])
            P.dma(lambda e, t=t, b=b: e.dma_start(out=yaT_d[:, :, t * 128:(t + 1) * 128].rearrange("c p t -> p c t"), in_=ytb[b]),
                  r=[f'ytb{b}'], w=[('yaT_d', t)], q='pool')
        P.barrier()
    def run_all():
        for l in range(L):
            src = x_in if l == 0 else h_d
            phase_norm(src, 2 * l)
            if cfg.stop == 'norm':
                return
            phase_proj(l)
            if cfg.stop == 'proj':
                return
            phase_attn()
            if cfg.stop == 'attn':
                return
            phase_gdn(l)
            if cfg.stop == 'gdn':
                return
            phase_merge(l)
            phase_wout(l, src)
            if cfg.stop == 'wout':
                return
            phase_norm(h_d, 2 * l + 1)
            phase_mlp(l)
            if cfg.stop == 'mlp':
                return
        phase_norm(h_d, 2 * DEPTH, final_out=y_out)

    run_all()
    P.barrier()
    P.emit(es)
    es.close()
    return nc, P, outs


def _t5_bucket(rel):
    half = 16
    exact = 8
    ret = np.where(rel > 0, half, 0)
    n = np.abs(rel)
    large = exact + (np.log(np.maximum(n, 1) / exact) / np.log(2048 / exact) * (half - exact)).astype(np.int32)
    large = np.minimum(large, half - 1)
    return (ret + np.where(n < exact, n, large)).astype(np.int32)


def host_consts():
    i = np.arange(128)
    eye = np.eye(128, dtype=np.float32)
    tril_i = (i[:, None] >= i[None, :]).astype(np.float32)
    tril_s = (i[:, None] > i[None, :]).astype(np.float32)
    triu_i = (i[:, None] <= i[None, :]).astype(np.float32)
    triu_s = (i[:, None] < i[None, :]).astype(np.float32)
    blk = lambda b: (i[:, None] // b == i[None, :] // b).astype(np.float32)
    bd16 = blk(16)
    cms = [blk(2 * b) - blk(b) for b in (16, 32, 64)]
    consts = np.stack([eye, tril_i, tril_s, triu_i, triu_s, bd16] + cms, axis=1).astype(np.float32)
    jmat = eye[::-1].copy()
    eoh = np.zeros((32, 3, 768), np.float32)
    amask = np.zeros((128, 3, 128), np.float32)
    for dl in range(3):
        rel_kq = 128 * (dl - 1) + i[:, None] - i[None, :]
        amask[:, dl, :] = np.where(np.abs(rel_kq) <= 64, 0.0, -30000.0)
        for g, dil in enumerate((1, 4, 16)):
            ii = np.arange(255)
            rel = 128 * (dl - 1) + ii - 127
            ok = np.abs(rel) <= 64
            b = _t5_bucket(rel * dil)
            eoh[b[ok], g, dl * 256 + ii[ok]] = 1.0
    return consts, jmat, eoh, amask


def make_in_map(x, keep, p):
    consts, jmat, eoh, amask = host_consts()
    m = {
        'x': np.ascontiguousarray(x, dtype=np.float32),
        'keep': np.full((128, 1), keep, np.float32),
        'consts': consts, 'jmat': jmat, 'eoh': eoh, 'amask': amask,
        'w_in0': np.ascontiguousarray(p['w_in'][0]), 'w_in1': np.ascontiguousarray(p['w_in'][1]),
        'w_br_a': p['w_br_a'], 'w_br_b': p['w_br_b'], 'w_br_c': p['w_br_c'], 'w_out': p['w_out'],
        'w_up': p['w_up'], 'w_down': p['w_down'],
        'norms': np.stack([p['norm_mix'][0], p['norm_mlp'][0], p['norm_mix'][1], p['norm_mlp'][1], p['norm_final']]).astype(np.float32),
        'conva': np.ascontiguousarray(p['conv_a'].reshape(DEPTH, 3, 24, 128).transpose(0, 3, 2, 1).reshape(DEPTH, 128, 72)),
        'convb': np.ascontiguousarray(p['conv_b'].reshape(DEPTH, 3, 8, 128).transpose(0, 3, 2, 1).reshape(DEPTH, 128, 24)),
        'gdnp': np.concatenate([p['a_log'].reshape(DEPTH, 16), p['dt_bias'].reshape(DEPTH, 16)], axis=1).astype(np.float32),
        'hn': np.ascontiguousarray(p['head_norm']),
        'relb': np.ascontiguousarray(p['rel_bias']),
    }
    return m


_CACHE = {}


def kernel(**inputs):
    p = {k: np.asarray(v, dtype=np.float32) for k, v in inputs.items()}
    xp, xs = p['x_prompt'], p['x_sample']
    T = 4096
    if 'nc' not in _CACHE:
        _CACHE['nc'] = build(Cfg(T=T))
    nc = _CACHE['nc'][0]
    jobs = {0: (xp[0], 1.0), 1: (xp[1], 1.0), 4: (xs[0:2].reshape(T, D), 0.0), 5: (xs[2:4].reshape(T, D), 0.0)}
    maps = [None] * 8
    for c, (xc, kp) in jobs.items():
        maps[c] = make_in_map(xc, kp, p)
    zmap = {k: np.zeros_like(v) for k, v in maps[0].items()}
    for c in range(8):
        if maps[c] is None:
            maps[c] = zmap
    res = run_bass_kernel_spmd(nc, maps, core_ids=list(range(8)))
    ys = [np.asarray(res.results[c]['y'], dtype=np.float32) for c in (0, 1, 4, 5)]
    y_prompt = np.stack([ys[0], ys[1]], axis=0)
    y_sample = np.concatenate([ys[2].reshape(2, 2048, D), ys[3].reshape(2, 2048, D)], axis=0)
    return (y_prompt, y_sample)
```

```python
import numpy as np
from contextlib import ExitStack
import concourse.bass as bass
import concourse.mybir as mybir
from concourse.bass_utils import run_bass_kernel_spmd

F32 = mybir.dt.float32
BF16 = mybir.dt.bfloat16
ALU = mybir.AluOpType
AF = mybir.ActivationFunctionType
AX = mybir.AxisListType

D = 2048
DEPTH = 2
EPS = 1e-6


class Prog:
    ENGS = ('pe', 'dve', 'act', 'pool', 'sp')
    NCH = 24
    EPOCH = 30000

    def __init__(self, nc):
        self.nc = nc
        self.ops = []
        self.last_w = {}
        self.readers = {}
        self._bar = 0
        self.marks = []

    def add(self, eng, fn, reads=(), writes=(), dma=False):
        idx = len(self.ops)
        writes = list(writes) + [k for k in reads if isinstance(k, str) and k.startswith('ps')]
        reads = [k for k in reads if not (isinstance(k, str) and k.startswith('ps'))]
        deps = set()
        for k in reads:
            if k in self.last_w:
                deps.add(self.last_w[k])
        for k in writes:
            if k in self.last_w:
                deps.add(self.last_w[k])
            deps.update(self.readers.get(k, ()))
        for k in reads:
            self.readers.setdefault(k, []).append(idx)
        for k in writes:
            self.last_w[k] = idx
            self.readers[k] = []
        deps.discard(idx)
        self.ops.append(dict(eng=eng, fn=fn, deps=deps, dma=dma, sig=dma))
        return idx

    def barrier(self):
        deps = set()
        lastc = {}
        for i in range(len(self.ops) - 1, self._bar - 1, -1):
            o = self.ops[i]
            if o['dma']:
                deps.add(i)
            elif o['eng'] not in lastc:
                lastc[o['eng']] = i
                deps.add(i)
        for i in range(self._bar - 1, -1, -1):
            o = self.ops[i]
            if not o['dma'] and o['eng'] not in lastc:
                lastc[o['eng']] = i
                deps.add(i)
            if len(lastc) == 4:
                break
        self.last_w = {}
        self.readers = {}
        for e in self.ENGS:
            self.ops.append(dict(eng=e, fn=lambda eng: None, deps=set(deps), dma=False, sig=False, bar=True))
        self._bar = len(self.ops)

    def pe(self, fn, r=(), w=()):
        return self.add('pe', fn, r, w)

    def dve(self, fn, r=(), w=()):
        return self.add('dve', fn, r, w)

    def act(self, fn, r=(), w=()):
        return self.add('act', fn, r, w)

    def pool(self, fn, r=(), w=()):
        return self.add('pool', fn, r, w)

    def dma(self, fn, r=(), w=(), q='sp'):
        return self.add(q, fn, r, w, dma=True)

    def emit(self, es):
        nc = self.nc
        ops = self.ops
        pos = {e: 0 for e in self.ENGS}
        for o in ops:
            o['pos'] = pos[o['eng']]
            pos[o['eng']] += 1
        for i, o in enumerate(ops):
            need = set()
            for d in o['deps']:
                od = ops[d]
                if od['eng'] == o['eng'] and not od['dma'] and not o.get('bar'):
                    if o['eng'] == 'pe':
                        continue
                    if o['eng'] in ('sp',):
                        continue
                    if o['pos'] - od['pos'] > 3 and not o['dma']:
                        continue
                need.add(d)
                od['sig'] = True
            o['need'] = need
        cnt = {e: 0 for e in self.ENGS}
        sems = {}

        def getsem(name):
            if name not in sems:
                sems[name] = es.enter_context(nc.semaphore(name))
            return sems[name]

        dcount = {e: 0 for e in self.ENGS}
        chcount = {}
        for o in ops:
            e = o['eng']
            if o['dma']:
                ch = dcount[e] % self.NCH
                dcount[e] += 1
                k = chcount.get((e, ch), 0)
                o['prev_ev'] = (getsem(f"d_{e}_{ch}"), 16 * k) if k > 0 else None
                chcount[(e, ch)] = k + 1
                assert 16 * (k + 1) < 32000, "too many DMAs per channel"
                o['ev'] = (getsem(f"d_{e}_{ch}"), 16 * (k + 1))
            elif o['sig']:
                c = cnt[e]
                ep = c // self.EPOCH
                o['ev'] = (getsem(f"c_{e}_{ep}"), c % self.EPOCH + 1)
                cnt[e] += 1
            else:
                o['ev'] = None
        by_eng = {e: [o for o in ops if o['eng'] == e] for e in self.ENGS}
        self.stats = {e: len(v) for e, v in by_eng.items()}

        def run(engname, eng):
            waited = {}
            for o in by_eng[engname]:
                evs = [ops[d]['ev'] for d in sorted(o['need'])]
                if o['dma'] and o['prev_ev'] is not None:
                    evs.append(o['prev_ev'])
                mx = {}
                for (s, v) in evs:
                    if v > mx.get(id(s), (None, 0))[1]:
                        mx[id(s)] = (s, v)
                for key, (s, v) in mx.items():
                    if waited.get(key, 0) >= v:
                        continue
                    eng.wait_ge(s, v)
                    waited[key] = v
                ins = o['fn'](eng)
                if o['ev'] is not None:
                    if ins is None:
                        ins = eng.nop()
                    ins.then_inc(o['ev'][0], 16 if o['dma'] else 1)

        with nc.Block() as block:
            @block.tensor
            def _(eng):
                run('pe', eng)

            @block.vector
            def _(eng):
                run('dve', eng)

            @block.scalar
            def _(eng):
                run('act', eng)

            @block.gpsimd
            def _(eng):
                run('pool', eng)

            @block.sync
            def _(eng):
                run('sp', eng)


A_W = 1024
B_W = 1024
C_W = 1536
IN_COLS = 17952
OFF_Q, OFF_K, OFF_V, OFF_Z, OFF_AB = 0, 1024, 2048, 3072, 4096
OFF_GB, OFF_GC, OFF_H = 4128, 5152, 6176
OFF_CQ, OFF_CK, OFF_CV, OFF_G = 7200, 8736, 10272, 11808
D_FF = 8192


class Cfg:
    def __init__(self, T=4096, depth=DEPTH, debug=(), stop=None):
        self.T = T
        self.depth = depth
        self.debug = tuple(debug)
        self.stop = stop


def build(cfg):
    T = cfg.T
    NT = T // 128
    TB = T // 2
    L = cfg.depth
    nc = bass.Bass("TRN2", target_bir_lowering=False)
    es = ExitStack()
    P = Prog(nc)
    dbg = set(cfg.debug)
    outs = []

    def dram(name, shape, dt, kind="Internal"):
        if name in dbg:
            kind = "ExternalOutput"
            outs.append(name)
        return nc.dram_tensor(name, shape, dt, kind=kind).ap()

    arena = es.enter_context(nc.sbuf_tensor("arena", [128, 52992], F32))
    psum = es.enter_context(nc.psum_tensor("psum", [128, 4096], F32))

    class Arena:
        def __init__(self):
            self.off = 0
            self.n = 0

        def alloc(self, free, dt, name=None):
            if isinstance(free, int):
                free = [free]
            n = 1
            for f in free:
                n *= f
            words = (n + 1) // 2 if dt == BF16 else n
            words = (words + 7) // 8 * 8
            assert self.off + words <= 52992, f"arena overflow {self.off + words}"
            ap = arena[:, self.off:self.off + words]
            self.off += words
            if dt == BF16:
                ap = ap.bitcast(BF16)
            ap = ap[:, 0:n]
            if len(free) == 2:
                ap = ap.rearrange("p (a b) -> p a b", a=free[0])
            elif len(free) == 3:
                ap = ap.rearrange("p (a b c) -> p a b c", a=free[0], b=free[1])
            self.n += 1
            return ap

    AR = Arena()

    def bank(i, dt=F32):
        ap = psum[:, i * 512:(i + 1) * 512]
        if dt == BF16:
            ap = ap.bitcast(BF16)
        return ap

    x_in = dram("x", [T, D], F32, "ExternalInput")
    keep_in = dram("keep", [128, 1], F32, "ExternalInput")
    consts_in = dram("consts", [128, 9, 128], F32, "ExternalInput")
    w_in = [dram(f"w_in{l}", [D, IN_COLS], F32, "ExternalInput") for l in range(DEPTH)]
    w_br_a = dram("w_br_a", [DEPTH, A_W, D], F32, "ExternalInput")
    w_br_b = dram("w_br_b", [DEPTH, B_W, D], F32, "ExternalInput")
    w_br_c = dram("w_br_c", [DEPTH, 512, D], F32, "ExternalInput")
    w_out = dram("w_out", [DEPTH, D, D], F32, "ExternalInput")
    w_up = dram("w_up", [DEPTH, D, D_FF], F32, "ExternalInput")
    w_down = dram("w_down", [DEPTH, D_FF, D], F32, "ExternalInput")
    norms_in = dram("norms", [2 * DEPTH + 1, D], F32, "ExternalInput")
    conva_in = dram("conva", [DEPTH, 128, 72], F32, "ExternalInput")
    convb_in = dram("convb", [DEPTH, 128, 24], F32, "ExternalInput")
    gdnp_in = dram("gdnp", [DEPTH, 32], F32, "ExternalInput")
    hn_in = dram("hn", [DEPTH, 128], F32, "ExternalInput")
    relb_in = dram("relb", [32, 12], F32, "ExternalInput")
    eoh_in = dram("eoh", [32, 3, 768], F32, "ExternalInput")
    amask_in = dram("amask", [128, 3, 128], F32, "ExternalInput")
    jmat_in = dram("jmat", [128, 128], F32, "ExternalInput")
    y_out = dram("y", [T, D], F32, "ExternalOutput")

    h_d = dram("h_d", [T, D], F32)
    xnT_d = dram("xnT_d", [16, 128, T + 2], BF16)
    qT_d = dram("qT_d", [8, 128, T], BF16)
    kT_d = dram("kT_d", [8, 128, T], BF16)
    vT_d = dram("vT_d", [8, 128, T], BF16)
    z_d = dram("z_d", [T, A_W], F32)
    ybT_d = dram("ybT_d", [8, 128, T], BF16)
    cqT_d = dram("cqT_d", [12, 128, T], BF16)
    ckT_d = dram("ckT_d", [12, 128, T], BF16)
    cv_d = dram("cv_d", [T, C_W], BF16)
    yaT_d = dram("yaT_d", [8, 128, T], BF16)
    ycT_d = dram("ycT_d", [4, 128, T], BF16)
    ab_d = dram("ab_d", [T, 32], F32)

    cst = AR.alloc([9, 128], F32)
    ident_f = cst[:, 0, :]
    ident_b = AR.alloc(128, BF16)
    ones_b = AR.alloc(128, BF16)
    ones_f = AR.alloc(128, F32)
    keepc = AR.alloc(1, F32)
    epsc = AR.alloc(1, F32)
    zero_b = AR.alloc(16, BF16)
    P.dma(lambda e: e.dma_start(out=cst, in_=consts_in), w=['cst'])
    P.dma(lambda e: e.dma_start(out=keepc, in_=keep_in), w=['keepc'])
    P.dve(lambda e: e.tensor_copy(ident_b, ident_f), r=['cst'], w=['ident_b'])
    P.dve(lambda e: e.memset(ones_b, 1.0), w=['ones_b'])
    P.dve(lambda e: e.memset(ones_f, 1.0), w=['ones_f'])
    P.dve(lambda e: e.memset(epsc, EPS), w=['epsc'])
    P.dve(lambda e: e.memset(zero_b, 0.0), w=['zero_b'])
    for col in (0, T + 1):
        P.dma(lambda e, col=col: e.dma_start(out=xnT_d[:, :, col:col + 1].rearrange("c p t -> p c t"),
                                             in_=zero_b.rearrange("p (c t) -> p c t", t=1), allow_slow_non_contiguous=True),
              r=['zero_b'], w=[('xnT_pad', col)])
    PERSIST = AR.off

    def wkey(i):
        return f'wt{i}'

    wcount = [0]

    def phase_norm(src_ap, nrow, final_out=None):
        P.marks.append(('phase_norm', len(P.ops)))
        AR.off = PERSIST
        nw = AR.alloc(D, F32)
        hbuf = [AR.alloc(D, F32) for _ in range(3)]
        sq = AR.alloc(D, F32)
        ssum = [AR.alloc(1, F32) for _ in range(3)]
        rstd = [AR.alloc(1, F32) for _ in range(3)]
        xnb = [AR.alloc(D, BF16) for _ in range(3)]
        xnTt = [AR.alloc([16, 128], BF16) for _ in range(3)]
        yo = [AR.alloc(D, F32) for _ in range(3)]
        P.dma(lambda e: e.dma_start(out=nw, in_=norms_in[nrow:nrow + 1, :].partition_broadcast(128)), w=['nw'])
        def ld(t):
            b = t % 3
            P.dma(lambda e, t=t, b=b: e.dma_start(out=hbuf[b], in_=src_ap[t * 128:(t + 1) * 128, :]),
                  w=[f'hbuf{b}'])
        ld(0)
        if NT > 1:
            ld(1)
        for t in range(NT):
            b = t % 3
            if t + 2 < NT:
                ld(t + 2)
            P.act(lambda e, b=b: e.activation(out=sq, in_=hbuf[b], func=AF.Square, accum_out=ssum[b]),
                  r=[f'hbuf{b}'], w=['sq', f'ssum{b}'])
            P.act(lambda e, b=b: e.activation(out=ssum[b], in_=ssum[b], func=AF.Sqrt, scale=1.0 / D, bias=epsc),
                  r=[f'ssum{b}', 'epsc'], w=[f'ssum{b}'])
            P.dve(lambda e, b=b: e.reciprocal(rstd[b], ssum[b]), r=[f'ssum{b}'], w=[f'rstd{b}'])
            if final_out is not None:
                P.dve(lambda e, b=b: e.scalar_tensor_tensor(out=yo[b], in0=hbuf[b], scalar=rstd[b], in1=nw,
                                                            op0=ALU.mult, op1=ALU.mult),
                      r=[f'hbuf{b}', f'rstd{b}', 'nw'], w=[f'yo{b}'])
                P.dma(lambda e, t=t, b=b: e.dma_start(out=final_out[t * 128:(t + 1) * 128, :], in_=yo[b]),
                      r=[f'yo{b}'], w=[('y_out', t)], q='pool')
                continue
            P.dve(lambda e, b=b: e.scalar_tensor_tensor(out=xnb[b], in0=hbuf[b], scalar=rstd[b], in1=nw,
                                                        op0=ALU.mult, op1=ALU.mult),
                  r=[f'hbuf{b}', f'rstd{b}', 'nw'], w=[f'xnb{b}'])
            for half in range(2):
                pb = bank(6 + half, BF16).rearrange("p (a b) -> p a b", a=8)
                for j in range(8):
                    c = half * 8 + j
                    P.pe(lambda e, b=b, c=c, j=j, pb=pb: e.transpose(pb[:, j, :], xnb[b][:, c * 128:(c + 1) * 128], ident_b),
                         r=[f'xnb{b}', 'ident_b'], w=[f'ps{6 + half}'])
                if half == 0:
                    P.dve(lambda e, b=b, pb=pb: e.tensor_copy(xnTt[b][:, 0:8, :], pb), r=['ps6'], w=[f'xnTt{b}a'])
                else:
                    P.act(lambda e, b=b, pb=pb: e.copy(xnTt[b][:, 8:16, :], pb), r=['ps7'], w=[f'xnTt{b}b'])
            P.dma(lambda e, t=t, b=b: e.dma_start(out=xnT_d[:, :, 1 + t * 128:1 + (t + 1) * 128].rearrange("c p t -> p c t"),
                                                  in_=xnTt[b]),
                  r=[f'xnTt{b}a', f'xnTt{b}b'], w=[('xnT_d', t)], q='pool')
        P.barrier()

    def load_xb(xb, tb):
        P.dma(lambda e: e.dma_start(out=xb, in_=xnT_d[:, :, tb * TB:tb * TB + TB + 2].rearrange("c p t -> p c t")),
              w=['xb'])
        col = TB + 1 if tb == 0 else 0
        P.dve(lambda e: e.tensor_scalar(xb[:, :, col:col + 1], xb[:, :, col:col + 1], keepc, None, ALU.mult),
              r=['xb', 'keepc'], w=['xb'])

    def load_w(wt, src2d, ncols, i):
        kc = src2d.shape[0] // 128
        P.dma(lambda e: e.dma_start(out=wt[:, 0:kc, 0:ncols], in_=src2d.rearrange("(c p) n -> p c n", p=128)),
              w=[wkey(i)], q='pool')

    def phase_proj(l):
        P.marks.append(('phase_proj', len(P.ops)))
        AR.off = PERSIST
        xb = AR.alloc([16, TB + 2], BF16)
        wts = [AR.alloc([16, 512], BF16) for _ in range(2)]
        cwa = AR.alloc(72, F32)
        cwb = AR.alloc(24, F32)
        st = [AR.alloc(512, F32) for _ in range(3)]
        sbf = [AR.alloc(512, BF16) for _ in range(2)]
        rn = AR.alloc(512, F32)
        sqb2 = [AR.alloc(512, BF16) for _ in range(2)]
        st1b = AR.alloc(512, F32)
        ab_sb = AR.alloc(32, F32)
        P.dma(lambda e: e.dma_start(out=cwa, in_=conva_in[l]), w=['cwa'])
        P.dma(lambda e: e.dma_start(out=cwb, in_=convb_in[l]), w=['cwb'])
        W = w_in[l]
        tiles = []
        s = 0
        while s < TB:
            n = min(510, TB - s)
            tiles.append((s, n))
            s += n
        oc = [0]
        pend = []
        pcnt = [0]

        def nxt_w():
            i = wcount[0] % 2
            wcount[0] += 1
            return i

        def fm_group(wi, wcol, x0, n, bk):
            for kc in range(16):
                P.pe(lambda e, kc=kc: e.matmul(bank(bk)[:, 0:n], wts[wi][:, kc, wcol:wcol + 128], xb[:, kc, x0:x0 + n],
                                               start=(kc == 0), stop=(kc == 15)),
                     r=[wkey(wi), 'xb'], w=[f'ps{bk}'])

        def out_dma(dst, src, rkey):
            P.dma(lambda e: e.dma_start(out=dst, in_=src), r=[rkey], w=[('pout', oc[0])])
            oc[0] += 1

        for tb in range(2):
            load_xb(xb, tb)
            t0 = tb * TB
            for wtile in range(6):
                wi = nxt_w()
                load_w(wts[wi], W[:, wtile * 512:(wtile + 1) * 512], 512, wi)
                for cg in range(4):
                    g = wtile * 4 + cg
                    kind, hd = g // 8, g % 8
                    for ti, (s, n) in enumerate(tiles):
                        bk = (g * len(tiles) + ti) % 4
                        fm_group(wi, cg * 128, s, n + 2, bk)
                        pre = bank(bk)
                        a = st[0]
                        P.dve(lambda e, pre=pre, n=n, g=g: e.tensor_scalar(a[:, 0:n], pre[:, 0:n], cwa[:, 3 * g:3 * g + 1], None, ALU.mult),
                              r=[f'ps{bk}', 'cwa'], w=['st0'])
                        P.dve(lambda e, pre=pre, n=n, g=g: e.scalar_tensor_tensor(out=a[:, 0:n], in0=pre[:, 1:n + 1], scalar=cwa[:, 3 * g + 1:3 * g + 2],
                                                                              in1=a[:, 0:n], op0=ALU.mult, op1=ALU.add),
                              r=[f'ps{bk}', 'cwa', 'st0'], w=['st0'])
                        P.dve(lambda e, pre=pre, n=n, g=g: e.scalar_tensor_tensor(out=a[:, 0:n], in0=pre[:, 2:n + 2], scalar=cwa[:, 3 * g + 2:3 * g + 3],
                                                                              in1=a[:, 0:n], op0=ALU.mult, op1=ALU.add),
                              r=[f'ps{bk}', 'cwa', 'st0'], w=['st0'])
                        ob = oc[0] % 2
                        dst = (qT_d, kT_d, vT_d)[kind][hd][:, t0 + s:t0 + s + n]
                        if kind == 2:
                            while pend:
                                pend.pop(0)()
                        elif len(pend) == 2:
                            pend.pop(0)()
                        if kind == 2:
                            P.act(lambda e, n=n, ob=ob: e.activation(out=sbf[ob][:, 0:n], in_=a[:, 0:n], func=AF.Silu),
                                  r=['st0'], w=[f'sbf{ob}'])
                        else:
                            pj = pcnt[0] % 2
                            pcnt[0] += 1
                            b1 = (st[1], st1b)[pj]
                            sqb = sqb2[pj]
                            P.act(lambda e, n=n, b1=b1: e.activation(out=b1[:, 0:n], in_=a[:, 0:n], func=AF.Silu),
                                  r=['st0'], w=[('st1', 'st1b')[pj]])
                            P.act(lambda e, n=n, b1=b1, sqb=sqb: e.activation(out=sqb[:, 0:n], in_=b1[:, 0:n], func=AF.Square),
                                  r=[('st1', 'st1b')[pj]], w=[f'sqb{pj}'])
                            def part2(n=n, ob=ob, kind=kind, dst=dst, b1=b1, sqb=sqb, pj=pj):
                                P.pe(lambda e: e.matmul(bank(4)[:, 0:n], ones_b, sqb[:, 0:n], start=True, stop=True),
                                     r=[f'sqb{pj}', 'ones_b'], w=['ps4'])
                                P.act(lambda e: e.activation(out=rn[:, 0:n], in_=bank(4)[:, 0:n], func=AF.Sqrt, bias=epsc),
                                      r=['ps4', 'epsc'], w=['rn'])
                                P.dve(lambda e: e.reciprocal(rn[:, 0:n], rn[:, 0:n]), r=['rn'], w=['rn'])
                                sc = (128.0 ** -0.5) if kind == 0 else 1.0
                                P.dve(lambda e: e.scalar_tensor_tensor(out=sbf[ob][:, 0:n], in0=b1[:, 0:n], scalar=sc, in1=rn[:, 0:n],
                                                                       op0=ALU.mult, op1=ALU.mult),
                                      r=[('st1', 'st1b')[pj], 'rn'], w=[f'sbf{ob}'])
                                P.dma(lambda e: e.dma_start(out=dst, in_=sbf[ob][:, 0:n]), r=[f'sbf{ob}'], w=[('pout2', n, ob, id(dst))])
                            pend.append(part2)
                            oc[0] += 1
                            continue
                        out_dma(dst, sbf[ob][:, 0:n], f'sbf{ob}')
            while pend:
                pend.pop(0)()
            for wtile in range(2):
                wi = nxt_w()
                load_w(wts[wi], W[:, OFF_Z + wtile * 512:OFF_Z + (wtile + 1) * 512], 512, wi)
                for t in range(TB // 128):
                    bk = t % 4
                    for kc in range(16):
                        P.pe(lambda e, kc=kc, t=t, bk=bk, wi=wi: e.matmul(bank(bk), xb[:, kc, 1 + t * 128:1 + (t + 1) * 128], wts[wi][:, kc, :],
                                                                   start=(kc == 0), stop=(kc == 15)),
                             r=[wkey(wi), 'xb'], w=[f'ps{bk}'])
                    sj = 1 + t % 2
                    P.act(lambda e, bk=bk, sj=sj: e.activation(out=st[sj], in_=bank(bk), func=AF.Silu),
                          r=[f'ps{bk}'], w=[f'st{sj}'])
                    out_dma(z_d[t0 + t * 128:t0 + (t + 1) * 128, wtile * 512:(wtile + 1) * 512], st[sj], f'st{sj}')
            wi = nxt_w()
            load_w(wts[wi], W[:, OFF_AB:OFF_AB + 32], 32, wi)
            for t in range(TB // 128):
                bk = t % 4
                for kc in range(16):
                    P.pe(lambda e, kc=kc, t=t, bk=bk, wi=wi: e.matmul(bank(bk)[:, 0:32], xb[:, kc, 1 + t * 128:1 + (t + 1) * 128], wts[wi][:, kc, 0:32],
                                                               start=(kc == 0), stop=(kc == 15)),
                         r=[wkey(wi), 'xb'], w=[f'ps{bk}'])
                P.act(lambda e, bk=bk: e.copy(ab_sb, bank(bk)[:, 0:32]), r=[f'ps{bk}'], w=['ab_sb'])
                out_dma(ab_d[t0 + t * 128:t0 + (t + 1) * 128, :], ab_sb, 'ab_sb')
            for j in range(8):
                wi = nxt_w()
                for q, off in enumerate((OFF_GB, OFF_GC, OFF_H)):
                    P.dma(lambda e, q=q, off=off, wi=wi, j=j: e.dma_start(out=wts[wi][:, :, q * 128:(q + 1) * 128],
                                                                     in_=W[:, off + j * 128:off + (j + 1) * 128].rearrange("(c p) n -> p c n", p=128)),
                          w=[wkey(wi)], q='pool')
                for ti, (s, n) in enumerate(tiles):
                    bg, bc, bh = ((0, 1, 2), (5, 6, 7))[(j * len(tiles) + ti) % 2]
                    fm_group(wi, 256, s, n + 2, bh)
                    fm_group(wi, 128, s, n + 2, bc)
                    fm_group(wi, 0, s, n + 2, bg)
                    hh = st[1]
                    pp = st[2]
                    a = st[0]
                    P.act(lambda e, n=n, bh=bh: e.copy(hh[:, 0:n + 2], bank(bh)[:, 0:n + 2]), r=[f'ps{bh}'], w=['st1'])
                    P.dve(lambda e, n=n, bc=bc: e.tensor_tensor(out=pp[:, 0:n + 2], in0=bank(bc)[:, 0:n + 2], in1=hh[:, 0:n + 2], op=ALU.mult),
                          r=[f'ps{bc}', 'st1'], w=['st2'])
                    P.dve(lambda e, n=n, j=j: e.tensor_scalar(a[:, 0:n], pp[:, 0:n], cwb[:, 3 * j:3 * j + 1], None, ALU.mult),
                          r=['st2', 'cwb'], w=['st0'])
                    P.dve(lambda e, n=n, j=j: e.scalar_tensor_tensor(out=a[:, 0:n], in0=pp[:, 1:n + 1], scalar=cwb[:, 3 * j + 1:3 * j + 2],
                                                                 in1=a[:, 0:n], op0=ALU.mult, op1=ALU.add),
                          r=['st2', 'cwb', 'st0'], w=['st0'])
                    P.dve(lambda e, n=n, j=j: e.scalar_tensor_tensor(out=a[:, 0:n], in0=pp[:, 2:n + 2], scalar=cwb[:, 3 * j + 2:3 * j + 3],
                                                                 in1=a[:, 0:n], op0=ALU.mult, op1=ALU.add),
                          r=['st2', 'cwb', 'st0'], w=['st0'])
                    ob = oc[0] % 2
                    P.dve(lambda e, n=n, ob=ob, bg=bg: e.tensor_tensor(out=sbf[ob][:, 0:n], in0=bank(bg)[:, 1:n + 1], in1=a[:, 0:n], op=ALU.mult),
                          r=[f'ps{bg}', 'st0'], w=[f'sbf{ob}'])
                    out_dma(ybT_d[j][:, t0 + s:t0 + s + n], sbf[ob][:, 0:n], f'sbf{ob}')
            for which, off, dstT, sc in ((0, OFF_CQ, cqT_d, 128.0 ** -0.5), (1, OFF_CK, ckT_d, 1.0)):
                for wtile in range(3):
                    wi = nxt_w()
                    load_w(wts[wi], W[:, off + wtile * 512:off + (wtile + 1) * 512], 512, wi)
                    for cg in range(4):
                        hd = wtile * 4 + cg
                        for tt in range(TB // 512):
                            bk = (cg * (TB // 512) + tt) % 4
                            fm_group(wi, cg * 128, 1 + tt * 512, 512, bk)
                            ob = oc[0] % 2
                            P.act(lambda e, bk=bk, ob=ob, sc=sc: e.activation(out=sbf[ob], in_=bank(bk), func=AF.Copy, scale=sc),
                                  r=[f'ps{bk}'], w=[f'sbf{ob}'])
                            out_dma(dstT[hd][:, t0 + tt * 512:t0 + (tt + 1) * 512], sbf[ob], f'sbf{ob}')
            for wtile in range(3):
                wi = nxt_w()
                load_w(wts[wi], W[:, OFF_CV + wtile * 512:OFF_CV + (wtile + 1) * 512], 512, wi)
                for t in range(TB // 128):
                    bk = t % 4
                    for kc in range(16):
                        P.pe(lambda e, kc=kc, t=t, bk=bk, wi=wi: e.matmul(bank(bk), xb[:, kc, 1 + t * 128:1 + (t + 1) * 128], wts[wi][:, kc, :],
                                                                   start=(kc == 0), stop=(kc == 15)),
                             r=[wkey(wi), 'xb'], w=[f'ps{bk}'])
                    ob = oc[0] % 2
                    P.dve(lambda e, bk=bk, ob=ob: e.tensor_copy(sbf[ob], bank(bk)), r=[f'ps{bk}'], w=[f'sbf{ob}'])
                    out_dma(cv_d[t0 + t * 128:t0 + (t + 1) * 128, wtile * 512:(wtile + 1) * 512], sbf[ob], f'sbf{ob}')
        P.barrier()
    mT_d = dram("mT_d", [16, 128, T], BF16)

    def phase_merge(l):
        P.marks.append(('phase_merge', len(P.ops)))
        AR.off = PERSIST
        xb = AR.alloc([16, TB + 2], BF16)
        wsets = [[AR.alloc([16, 256], BF16) for _ in range(4)] + [AR.alloc([4, 256], BF16)] for _ in range(2)]
        ya = AR.alloc([20, 512], BF16)
        sg = [AR.alloc(512, F32) for _ in range(3)]
        t1 = AR.alloc(512, F32)
        t2 = AR.alloc(512, F32)
        mo = [AR.alloc(512, BF16) for _ in range(2)]
        W = w_in[l]
        oc = [0]

        def load_ws(it):
            si = it % 2
            ws = wsets[si]
            c0w = (it % 8) * 256
            for b3 in range(3):
                P.dma(lambda e, b3=b3: e.dma_start(out=ws[b3], in_=W[:, OFF_G + b3 * D + c0w:OFF_G + b3 * D + c0w + 256].rearrange("(c p) n -> p c n", p=128)),
                      w=[f'mw{b3}_{si}'], q='pool')
            P.dma(lambda e: e.dma_start(out=ws[3][:, 0:8, :], in_=w_br_a[l][:, c0w:c0w + 256].rearrange("(c p) n -> p c n", p=128)),
                  w=[f'mw3a_{si}'], q='pool')
            P.dma(lambda e: e.dma_start(out=ws[3][:, 8:16, :], in_=w_br_b[l][:, c0w:c0w + 256].rearrange("(c p) n -> p c n", p=128)),
                  w=[f'mw3b_{si}'], q='pool')
            P.dma(lambda e: e.dma_start(out=ws[4], in_=w_br_c[l][:, c0w:c0w + 256].rearrange("(c p) n -> p c n", p=128)),
                  w=[f'mw4_{si}'], q='pool')

        load_ws(0)
        for it in range(16):
            tb, fcg = it // 8, it % 8
            if fcg == 0:
                load_xb(xb, tb)
            if it + 1 < 16:
                load_ws(it + 1)
            t0 = tb * TB
            si = it % 2
            ws = wsets[si]
            for tt in range(TB // 512):
                c0 = t0 + tt * 512
                P.dma(lambda e, c0=c0: e.dma_start(out=ya[:, 0:8, :], in_=yaT_d[:, :, c0:c0 + 512].rearrange("c p t -> p c t")), w=['ya_a'])
                P.dma(lambda e, c0=c0: e.dma_start(out=ya[:, 8:16, :], in_=ybT_d[:, :, c0:c0 + 512].rearrange("c p t -> p c t")), w=['ya_b'])
                P.dma(lambda e, c0=c0: e.dma_start(out=ya[:, 16:20, :], in_=ycT_d[:, :, c0:c0 + 512].rearrange("c p t -> p c t")), w=['ya_c'])
                for f2 in range(2):
                    fc = fcg * 2 + f2
                    cs = slice(f2 * 128, (f2 + 1) * 128)
                    for b3 in range(3):
                        for kc in range(16):
                            P.pe(lambda e, b3=b3, kc=kc, cs=cs, tt=tt, ws=ws: e.matmul(bank(b3), ws[b3][:, kc, cs], xb[:, kc, 1 + tt * 512:1 + (tt + 1) * 512],
                                                                                      start=(kc == 0), stop=(kc == 15)),
                                 r=[f'mw{b3}_{si}', 'xb'], w=[f'ps{b3}'])
                    for kc in range(8):
                        P.pe(lambda e, kc=kc, cs=cs, ws=ws: e.matmul(bank(3), ws[3][:, kc, cs], ya[:, kc, :], start=(kc == 0), stop=(kc == 7)),
                             r=[f'mw3a_{si}', 'ya_a'], w=['ps3'])
                    for kc in range(8):
                        P.pe(lambda e, kc=kc, cs=cs, ws=ws: e.matmul(bank(4), ws[3][:, 8 + kc, cs], ya[:, 8 + kc, :], start=(kc == 0), stop=(kc == 7)),
                             r=[f'mw3b_{si}', 'ya_b'], w=['ps4'])
                    for kc in range(4):
                        P.pe(lambda e, kc=kc, cs=cs, ws=ws: e.matmul(bank(5), ws[4][:, kc, cs], ya[:, 16 + kc, :], start=(kc == 0), stop=(kc == 3)),
                             r=[f'mw4_{si}', 'ya_c'], w=['ps5'])
                    for b3 in range(3):
                        P.act(lambda e, b3=b3: e.activation(out=sg[b3], in_=bank(b3), func=AF.Sigmoid), r=[f'ps{b3}'], w=[f'sg{b3}'])
                    ob = oc[0] % 2
                    oc[0] += 1
                    P.dve(lambda e: e.tensor_tensor(out=t1, in0=bank(3), in1=sg[0], op=ALU.mult), r=['ps3', 'sg0'], w=['t1'])
                    P.dve(lambda e: e.tensor_tensor(out=t2, in0=bank(4), in1=sg[1], op=ALU.mult), r=['ps4', 'sg1'], w=['t2'])
                    P.dve(lambda e: e.tensor_tensor(out=t1, in0=t1, in1=t2, op=ALU.add), r=['t1', 't2'], w=['t1'])
                    P.dve(lambda e: e.tensor_tensor(out=t2, in0=bank(5), in1=sg[2], op=ALU.mult), r=['ps5', 'sg2'], w=['t2'])
                    P.dve(lambda e, ob=ob: e.tensor_tensor(out=mo[ob], in0=t1, in1=t2, op=ALU.add), r=['t1', 't2'], w=[f'mo{ob}'])
                    P.dma(lambda e, ob=ob, fc=fc, c0=c0: e.dma_start(out=mT_d[fc][:, c0:c0 + 512], in_=mo[ob]),
                          r=[f'mo{ob}'], w=[('mT_d', oc[0])], q='pool')
        P.barrier()

    def phase_wout(l, h_src):
        P.marks.append(('phase_wout', len(P.ops)))
        AR.off = PERSIST
        xb = AR.alloc([16, TB], BF16)
        wts = [AR.alloc([16, 512], BF16) for _ in range(2)]
        hb = [AR.alloc(512, F32) for _ in range(3)]
        oc = [0]
        for tb in range(2):
            t0 = tb * TB
            P.dma(lambda e, t0=t0: e.dma_start(out=xb, in_=mT_d[:, :, t0:t0 + TB].rearrange("c p t -> p c t")), w=['xb'])
            for og in range(4):
                wi = og % 2
                P.dma(lambda e, og=og, wi=wi: e.dma_start(out=wts[wi], in_=w_out[l][:, og * 512:(og + 1) * 512].rearrange("(c p) n -> p c n", p=128)),
                      w=[wkey(wi)], q='pool')
                for t in range(TB // 128):
                    bk = t % 4
                    hi = oc[0] % 3
                    oc[0] += 1
                    r0 = t0 + t * 128
                    P.dma(lambda e, r0=r0, og=og, hi=hi: e.dma_start(out=hb[hi], in_=h_src[r0:r0 + 128, og * 512:(og + 1) * 512]), w=[f'hb{hi}'])
                    for kc in range(16):
                        P.pe(lambda e, kc=kc, t=t, bk=bk, wi=wi: e.matmul(bank(bk), xb[:, kc, t * 128:(t + 1) * 128], wts[wi][:, kc, :],
                                                                          start=(kc == 0), stop=(kc == 15)),
                             r=[wkey(wi), 'xb'], w=[f'ps{bk}'])
                    P.dve(lambda e, bk=bk, hi=hi: e.tensor_tensor(out=hb[hi], in0=bank(bk), in1=hb[hi], op=ALU.add),
                          r=[f'ps{bk}', f'hb{hi}'], w=[f'hb{hi}'])
                    P.dma(lambda e, r0=r0, og=og, hi=hi: e.dma_start(out=h_d[r0:r0 + 128, og * 512:(og + 1) * 512], in_=hb[hi]),
                          r=[f'hb{hi}'], w=[('h_d', oc[0])])
        P.barrier()

    def phase_mlp(l):
        P.marks.append(('phase_mlp', len(P.ops)))
        AR.off = PERSIST
        TM = 512
        xb = AR.alloc([16, TM], BF16)
        hT = AR.alloc([64, TM], BF16)
        wts = [AR.alloc([16, 512], BF16) for _ in range(3)]
        rl = [AR.alloc(512, F32) for _ in range(2)]
        hb = [AR.alloc(512, F32) for _ in range(4)]
        wc = [0]
        oc = [0]
        for tm in range(T // TM):
            t0 = tm * TM
            P.dma(lambda e, t0=t0: e.dma_start(out=xb, in_=xnT_d[:, :, 1 + t0:1 + t0 + TM].rearrange("c p t -> p c t")), w=['xb'])
            for ut in range(16):
                wi = wc[0] % 3
                wc[0] += 1
                P.dma(lambda e, ut=ut, wi=wi: e.dma_start(out=wts[wi], in_=w_up[l][:, ut * 512:(ut + 1) * 512].rearrange("(c p) n -> p c n", p=128)),
                      w=[wkey(wi)], q='pool')
                for cg in range(4):
                    bk = 4 + (ut * 4 + cg) % 4
                    for kc in range(16):
                        P.pe(lambda e, kc=kc, cg=cg, bk=bk, wi=wi: e.matmul(bank(bk), wts[wi][:, kc, cg * 128:(cg + 1) * 128], xb[:, kc, :],
                                                                            start=(kc == 0), stop=(kc == 15)),
                             r=[wkey(wi), 'xb'], w=[f'ps{bk}'])
                    ri = (ut * 4 + cg) % 2
                    P.act(lambda e, bk=bk, ri=ri: e.activation(out=rl[ri], in_=bank(bk), func=AF.Relu), r=[f'ps{bk}'], w=[f'rl{ri}'])
                    P.dve(lambda e, ri=ri, ut=ut, cg=cg: e.tensor_tensor(out=hT[:, ut * 4 + cg, :], in0=rl[ri], in1=rl[ri], op=ALU.mult),
                          r=[f'rl{ri}'], w=[('hT', ut * 4 + cg)])
            for og in range(4):
                for kt in range(4):
                    wi = wc[0] % 3
                    wc[0] += 1
                    P.dma(lambda e, og=og, kt=kt, wi=wi: e.dma_start(out=wts[wi], in_=w_down[l][kt * 2048:(kt + 1) * 2048, og * 512:(og + 1) * 512].rearrange("(c p) n -> p c n", p=128)),
                          w=[wkey(wi)], q='pool')
                    for t in range(4):
                        for kc in range(16):
                            P.pe(lambda e, kc=kc, kt=kt, t=t, wi=wi: e.matmul(bank(t), hT[:, kt * 16 + kc, t * 128:(t + 1) * 128], wts[wi][:, kc, :],
                                                                              start=(kt == 0 and kc == 0), stop=(kt == 3 and kc == 15)),
                                 r=[wkey(wi), ('hT', kt * 16 + kc)], w=[f'ps{t}'])
                for t in range(4):
                    hi = oc[0] % 4
                    oc[0] += 1
                    r0 = t0 + t * 128
                    P.dma(lambda e, r0=r0, og=og, hi=hi: e.dma_start(out=hb[hi], in_=h_d[r0:r0 + 128, og * 512:(og + 1) * 512]),
                          r=[('h_d2', r0, og)], w=[f'hb{hi}'])
                    P.dve(lambda e, t=t, hi=hi: e.tensor_tensor(out=hb[hi], in0=bank(t), in1=hb[hi], op=ALU.add),
                          r=[f'ps{t}', f'hb{hi}'], w=[f'hb{hi}'])
                    P.dma(lambda e, r0=r0, og=og, hi=hi: e.dma_start(out=h_d[r0:r0 + 128, og * 512:(og + 1) * 512], in_=hb[hi]),
                          r=[f'hb{hi}'], w=[('h_d2', r0, og)])
        P.barrier()
    oext_d = dram("oext_d", [3, T, 516], F32)
    rv_d = dram("rv_d", [3, 12, 768], F32)
    AR.off = PERSIST
    BT = AR.alloc([36, 128], F32)
    amask = AR.alloc([3, 128], F32)
    jmat = AR.alloc(128, F32)
    P.dma(lambda e: e.dma_start(out=amask, in_=amask_in), w=['amask'])
    P.dma(lambda e: e.dma_start(out=jmat, in_=jmat_in), w=['jmat'])
    PERSIST = AR.off

    def setup_bias():
        relb = AR.alloc(12, F32)
        eoh = AR.alloc([3, 768], F32)
        rvs = AR.alloc(768, F32)
        hs = [AR.alloc(128, F32) for _ in range(2)]
        P.dma(lambda e: e.dma_start(out=relb[0:32, :], in_=relb_in), w=['relb'])
        P.dma(lambda e: e.dma_start(out=eoh[0:32], in_=eoh_in), w=['eoh'])
        for g in range(3):
            for half in range(2):
                P.pe(lambda e, g=g, half=half: e.matmul(bank(half)[0:12, 0:384], relb[0:32, :], eoh[0:32, g, half * 384:(half + 1) * 384], start=True, stop=True),
                     r=['relb', 'eoh'], w=[f'ps{half}'])
                P.dve(lambda e, half=half: e.tensor_copy(rvs[0:12, half * 384:(half + 1) * 384], bank(half)[0:12, 0:384]), r=[f'ps{half}'], w=['rvs'])
            P.dma(lambda e, g=g: e.dma_start(out=rv_d[g], in_=rvs[0:12, :]), r=['rvs'], w=[('rv_d', g)])
        n = 0
        for g in range(3):
            for j in range(4):
                for dl in range(3):
                    hi = n % 2
                    src = rv_d[g, g * 4 + j, dl * 256:dl * 256 + 128]
                    hap = bass.AP(tensor=src.tensor, offset=src.offset, ap=[[1, 128], [1, 128]])
                    P.dma(lambda e, hap=hap, hi=hi: e.dma_start(out=hs[hi], in_=hap), r=[('rv_d', g)], w=[f'hs{hi}'])
                    bk = 2 + n % 2
                    P.pe(lambda e, hi=hi, bk=bk: e.matmul(bank(bk)[:, 0:128], hs[hi], jmat, start=True, stop=True), r=[f'hs{hi}', 'jmat'], w=[f'ps{bk}'])
                    P.dve(lambda e, bk=bk, idx=(g * 4 + j) * 3 + dl, dl=dl: e.tensor_tensor(out=BT[:, idx, :], in0=bank(bk)[:, 0:128], in1=amask[:, dl, :], op=ALU.add),
                          r=[f'ps{bk}', 'amask'], w=['BT'])
                    n += 1
        P.barrier()

    setup_bias()

    def phase_attn():
        P.marks.append(('phase_attn', len(P.ops)))
        AR.off = PERSIST
        QT4 = AR.alloc([4, T], BF16)
        KT4 = AR.alloc([4, T], BF16)
        vt = AR.alloc([NT, 4, 130], BF16)
        sT = [AR.alloc([3, 128], F32) for _ in range(4)]
        pT = [AR.alloc([3, 128], BF16) for _ in range(4)]
        ucnt = [0]
        ob = [AR.alloc([4, 129], F32) for _ in range(2)]
        P.dve(lambda e: e.memset(vt[:, :, :, 128:129], 1.0), w=['vt_ones'])
        n = 0
        for g, d in enumerate((1, 4, 16)):
            Ls = T // d
            NTs = Ls // 128
            P.dma(lambda e, g=g: e.dma_start(out=QT4, in_=cqT_d[g * 4:(g + 1) * 4].rearrange("h p t -> p h t")), w=['QT4'])
            P.dma(lambda e, g=g: e.dma_start(out=KT4, in_=ckT_d[g * 4:(g + 1) * 4].rearrange("h p t -> p h t")), w=['KT4'])
            cvr = cv_d.rearrange("(n dd) c -> dd n c", dd=d)
            oxr = oext_d[g].rearrange("(n dd) c -> dd n c", dd=d)
            for r in range(d):
                for kt in range(NTs):
                    P.dma(lambda e, r=r, g=g, kt=kt, cvr=cvr: e.dma_start(
                        out=vt[:, kt, :, 0:128],
                        in_=cvr[r][kt * 128:(kt + 1) * 128, g * 512:(g + 1) * 512].rearrange("i (j c) -> i j c", j=4)),
                        r=['vt_ones'], w=[('vt', kt)])
                units = [(qt, jp) for qt in range(NTs) for jp in range(2)]

                def stage_s(ui, qt, jp, g=g, d=d, r=r, NTs=NTs):
                    up = ui % 2
                    dls = [dl for dl in range(3) if 0 <= qt + dl - 1 < NTs]
                    d0, d1 = dls[0], dls[-1] + 1
                    q0 = r + d * qt * 128
                    qs = slice(q0, q0 + d * 127 + 1, d)
                    for jj in range(2):
                        j = 2 * jp + jj
                        bk = 2 * up + jj
                        sb_i = 2 * up + jj
                        for dl in dls:
                            k0 = r + d * (qt + dl - 1) * 128
                            ks = slice(k0, k0 + d * 127 + 1, d)
                            P.pe(lambda e, bk=bk, dl=dl, j=j, ks=ks: e.matmul(bank(bk)[:, dl * 128:(dl + 1) * 128], KT4[:, j, ks], QT4[:, j, qs], start=True, stop=True),
                                 r=['KT4', 'QT4'], w=[f'ps{bk}'])
                    for jj in range(2):
                        j = 2 * jp + jj
                        bk = 2 * up + jj
                        sb_i = 2 * up + jj
                        bidx = (g * 4 + j) * 3
                        P.dve(lambda e, bk=bk, bidx=bidx, sb_i=sb_i: e.tensor_tensor(
                            out=sT[sb_i][:, d0:d1, :], in0=bank(bk)[:, d0 * 128:d1 * 128].rearrange("p (a b) -> p a b", b=128),
                            in1=BT[:, bidx + d0:bidx + d1, :], op=ALU.add),
                            r=[f'ps{bk}', 'BT'], w=[f'sT{sb_i}'])
                        P.act(lambda e, sb_i=sb_i: e.activation(out=pT[sb_i][:, d0:d1, :], in_=sT[sb_i][:, d0:d1, :], func=AF.Exp),
                              r=[f'sT{sb_i}'], w=[f'pT{sb_i}'])
                        cross = None
                        if qt == NTs // 2 - 1:
                            cross = 2
                        elif qt == NTs // 2:
                            cross = 0
                        if cross is not None and cross in dls:
                            P.dve(lambda e, cross=cross, sb_i=sb_i: e.tensor_scalar(pT[sb_i][:, cross, :], pT[sb_i][:, cross, :], keepc, None, ALU.mult),
                                  r=[f'pT{sb_i}', 'keepc'], w=[f'pT{sb_i}'])

                def stage_pv(ui, qt, jp, g=g, d=d, r=r, NTs=NTs, oxr=oxr):
                    up = ui % 2
                    o2 = qt % 2
                    dls = [dl for dl in range(3) if 0 <= qt + dl - 1 < NTs]
                    obk = 4 + up
                    for jj in range(2):
                        j = 2 * jp + jj
                        sb_i = 2 * up + jj
                        oc0 = jj * 129
                        for dl in dls:
                            kt = qt + dl - 1
                            P.pe(lambda e, oc0=oc0, dl=dl, kt=kt, j=j, sb_i=sb_i, first=(dl == dls[0]), last=(dl == dls[-1]):
                                 e.matmul(bank(obk)[:, oc0:oc0 + 129], pT[sb_i][:, dl, :], vt[:, kt, j, 0:129], start=first, stop=last),
                                 r=[f'pT{sb_i}', ('vt', kt), 'vt_ones'], w=[f'ps{obk}'])
                    P.act(lambda e: e.copy(ob[o2][:, jp * 2:jp * 2 + 2, :], bank(obk)[:, 0:258].rearrange("p (a b) -> p a b", b=129)),
                          r=[f'ps{obk}'], w=[f'ob{o2}_{jp}'])
                    if jp == 1:
                        P.dma(lambda e: e.dma_start(out=oxr[r][qt * 128:(qt + 1) * 128, :], in_=ob[o2].rearrange("p a b -> p (a b)")),
                              r=[f'ob{o2}_0', f'ob{o2}_1'], w=[('oext', g, r, qt)], q='pool')

                stage_s(ucnt[0], *units[0])
                for ui, (qt, jp) in enumerate(units):
                    if ui + 1 < len(units):
                        stage_s(ucnt[0] + ui + 1, *units[ui + 1])
                    stage_pv(ucnt[0] + ui, qt, jp)
                ucnt[0] += len(units)
        P.barrier()
        AR.off = PERSIST
        og = [[AR.alloc([4, 129], F32) for _ in range(3)] for _ in range(2)]
        rec = [AR.alloc(4, F32) for _ in range(2)]
        yb = [AR.alloc([4, 128], BF16) for _ in range(2)]
        ytb = [AR.alloc([4, 128], BF16) for _ in range(2)]
        for t in range(NT):
            b = t % 2
            for g in range(3):
                P.dma(lambda e, g=g, t=t, b=b: e.dma_start(out=og[b][g].rearrange("p a b -> p (a b)"), in_=oext_d[g][t * 128:(t + 1) * 128, :]), w=[f'og{b}{g}'])
            P.dve(lambda e, b=b: e.tensor_tensor(out=og[b][0], in0=og[b][0], in1=og[b][1], op=ALU.add), r=[f'og{b}0', f'og{b}1'], w=[f'og{b}0'])
            P.dve(lambda e, b=b: e.tensor_tensor(out=og[b][0], in0=og[b][0], in1=og[b][2], op=ALU.add), r=[f'og{b}0', f'og{b}2'], w=[f'og{b}0'])
            P.dve(lambda e, b=b: e.reciprocal(rec[b], og[b][0][:, :, 128]), r=[f'og{b}0'], w=[f'rec{b}'])
            P.dve(lambda e, b=b: e.tensor_tensor(out=yb[b], in0=og[b][0][:, :, 0:128], in1=rec[b].unsqueeze(2).to_broadcast([128, 4, 128]), op=ALU.mult),
                  r=[f'og{b}0', f'rec{b}'], w=[f'yb{b}'])
            pb = bank(6 + b, BF16).rearrange("p (a b) -> p a b", b=128)
            for j in range(4):
                P.pe(lambda e, b=b, j=j, pb=pb: e.transpose(pb[:, j, :], yb[b][:, j, :], ident_b), r=[f'yb{b}', 'ident_b'], w=[f'ps{6 + b}'])
            P.act(lambda e, b=b, pb=pb: e.copy(ytb[b], pb[:, 0:4, :]), r=[f'ps{6 + b}'], w=[f'ytb{b}'])
            P.dma(lambda e, t=t, b=b: e.dma_start(out=ycT_d[:, :, t * 128:(t + 1) * 128].rearrange("c p t -> p c t"), in_=ytb[b]),
                  r=[f'ytb{b}'], w=[('ycT_d', t)], q='pool')
        P.barrier()
    of_d = dram("of_d", [2, 8, T, 128], F32)

    def phase_gdn(l):
        P.marks.append(('phase_gdn', len(P.ops)))
        AR.off = PERSIST
        ab = AR.alloc([NT, 32], F32)
        gp = AR.alloc(32, F32)
        negA = AR.alloc(16, F32)
        gg = AR.alloc([NT, 16], F32)
        bt = AR.alloc([NT, 16], F32)
        nbt = AR.alloc([NT, 16], F32)
        gc = AR.alloc([NT, 16], F32)
        gtot = AR.alloc([NT, 16], F32)
        egt = AR.alloc([NT, 16], F32)
        kds = AR.alloc([NT, 16], F32)
        bw = AR.alloc([NT, 16], F32)
        S32 = AR.alloc([16, 128], F32)
        Sbf = AR.alloc([16, 2, 128], BF16)
        tril_i, tril_s, triu_i, triu_s = cst[:, 1, :], cst[:, 2, :], cst[:, 3, :], cst[:, 4, :]
        P.dma(lambda e: e.dma_start(out=ab, in_=ab_d.rearrange("(t p) c -> p t c", p=128)), w=['ab'])
        P.dma(lambda e: e.dma_start(out=gp, in_=gdnp_in[l:l + 1, :].partition_broadcast(128)), w=['gp'])
        P.act(lambda e: e.activation(out=negA, in_=gp[:, 0:16], func=AF.Exp), r=['gp'], w=['negA'])
        P.dve(lambda e: e.tensor_scalar(negA, negA, -1.0, None, ALU.mult), r=['negA'], w=['negA'])
        P.dve(lambda e: e.tensor_tensor(out=gg, in0=ab[:, :, 0:16], in1=gp[:, 16:32].unsqueeze(1).to_broadcast([128, NT, 16]), op=ALU.add),
              r=['ab', 'gp'], w=['gg'])
        P.act(lambda e: e.activation(out=gg, in_=gg, func=AF.Exp), r=['gg'], w=['gg'])
        P.act(lambda e: e.activation(out=gg, in_=gg, func=AF.Ln, bias=1.0), r=['gg'], w=['gg'])
        P.dve(lambda e: e.tensor_tensor(out=gg, in0=gg, in1=negA.unsqueeze(1).to_broadcast([128, NT, 16]), op=ALU.mult),
              r=['gg', 'negA'], w=['gg'])
        P.act(lambda e: e.activation(out=bt, in_=ab[:, :, 16:32], func=AF.Sigmoid), r=['ab'], w=['bt'])
        P.dve(lambda e: e.tensor_scalar(nbt, bt, -1.0, None, ALU.mult), r=['bt'], w=['nbt'])
        for t in range(NT):
            P.pe(lambda e, t=t: e.matmul(bank(0)[:, t * 16:t * 16 + 8], triu_i, gg[:, t, 0:8], start=True, stop=True), r=['gg', 'cst'], w=['ps0'])
            P.pe(lambda e, t=t: e.matmul(bank(0)[:, t * 16 + 8:t * 16 + 16], tril_i, gg[:, t, 8:16], start=True, stop=True), r=['gg', 'cst'], w=['ps0'])
            P.pe(lambda e, t=t: e.matmul(bank(1)[:, t * 16:t * 16 + 16], ones_f, gg[:, t, :], start=True, stop=True), r=['gg', 'ones_f'], w=['ps1'])
        P.dve(lambda e: e.tensor_copy(gc.rearrange("p a b -> p (a b)"), bank(0)[:, 0:NT * 16]), r=['ps0'], w=['gc'])
        P.dve(lambda e: e.tensor_copy(gtot.rearrange("p a b -> p (a b)"), bank(1)[:, 0:NT * 16]), r=['ps1'], w=['gtot'])
        P.act(lambda e: e.activation(out=egt, in_=gtot, func=AF.Exp), r=['gtot'], w=['egt'])
        P.dve(lambda e: e.tensor_tensor(out=kds, in0=gtot, in1=gc, op=ALU.subtract), r=['gtot', 'gc'], w=['kds'])
        P.act(lambda e: e.activation(out=kds, in_=kds, func=AF.Exp), r=['kds'], w=['kds'])
        P.act(lambda e: e.activation(out=bw, in_=gc, func=AF.Exp), r=['gc'], w=['bw'])
        P.dve(lambda e: e.tensor_tensor(out=bw, in0=bw, in1=bt, op=ALU.mult), r=['bw', 'bt'], w=['bw'])
        P.dve(lambda e: e.memset(S32, 0.0), w=['S32'])
        P.dve(lambda e: e.memset(Sbf, 0.0), w=['Sbf'])
        NS = 8
        slots = []
        for s in range(NS):
            d_ = dict(
                qkv=AR.alloc([3, 128], BF16), gU=AR.alloc(128, F32), Em=AR.alloc(128, F32), E=AR.alloc(128, F32),
                egrow=AR.alloc(128, F32), ES=AR.alloc(128, F32), EI=AR.alloc(128, F32),
                N=[AR.alloc([2, 128], BF16) for _ in range(2)], intra=AR.alloc(128, BF16), intraT=AR.alloc(128, BF16),
                qg=AR.alloc(128, BF16), kdec=AR.alloc(128, BF16), X32=AR.alloc(256, F32), Xb=AR.alloc(256, BF16), wlo=AR.alloc(128, BF16), Pm=[AR.alloc([2, 128], BF16) for _ in range(2)], Cn=AR.alloc([2, 128], BF16), M1=AR.alloc([2, 128], BF16),
                wT=AR.alloc([2, 128], BF16), vnew=AR.alloc(128, BF16), osb=AR.alloc(128, F32))
            slots.append(d_)

        def chain_step(c, h, dr, s):
            B = slots[s]
            K = lambda nm: f'g{s}_{nm}'
            b0, b1, b2, b3 = s, s, s, s
            pb0, pb2, pb3 = bank(b0), bank(b2), bank(b3)
            pb1 = bank(b1, BF16)
            j16 = dr * 8 + h
            Ud = triu_i if dr == 0 else tril_i
            mS = tril_s if dr == 0 else triu_s
            mI = tril_i if dr == 0 else triu_i
            cs = slice(c * 128, (c + 1) * 128)
            qc, kc_, vc = B['qkv'][:, 0, :], B['qkv'][:, 1, :], B['qkv'][:, 2, :]
            for i3, src in enumerate((qT_d, kT_d, vT_d)):
                P.dma(lambda e, i3=i3, src=src: e.dma_start(out=B['qkv'][:, i3, :], in_=src[h][:, cs]), w=[K(f'qkv{i3}')])
            qk = [K('qkv0'), K('qkv1'), K('qkv2')]
            P.pe(lambda e: e.matmul(pb0[:, 0:128], kc_, kc_, start=True, stop=True), r=[qk[1]], w=[f'ps{b0}'])
            P.pe(lambda e: e.matmul(pb0[:, 128:256], qc, kc_, start=True, stop=True), r=[qk[0], qk[1]], w=[f'ps{b0}'])
            P.dve(lambda e: e.tensor_scalar(B['gU'], Ud, gg[:, c, j16:j16 + 1], None, ALU.mult), r=['gg', 'cst'], w=[K('gU')])
            P.pe(lambda e: e.matmul(pb0[:, 256:384], ones_f, B['gU'], start=True, stop=True), r=[K('gU'), 'ones_f'], w=[f'ps{b0}'])
            P.pe(lambda e: e.transpose(pb1[:, 768:896], kc_, ident_b), r=[qk[1], 'ident_b'], w=[f'ps{b1}'])
            P.pe(lambda e: e.transpose(pb1[:, 896:1024], vc, ident_b), r=[qk[2], 'ident_b'], w=[f'ps{b1}'])
            yield
            P.dve(lambda e: e.tensor_scalar(B['Em'], pb0[:, 256:384], gc[:, c, j16:j16 + 1], 0.0, ALU.subtract, ALU.max),
                  r=[f'ps{b0}', 'gc'], w=[K('Em')])
            P.act(lambda e: e.activation(out=B['E'], in_=B['Em'], func=AF.Exp, scale=-1.0), r=[K('Em')], w=[K('E')])
            P.act(lambda e: e.activation(out=B['egrow'], in_=pb0[:, 256:384], func=AF.Exp), r=[f'ps{b0}'], w=[K('egrow')])
            P.dve(lambda e: e.tensor_tensor(out=B['ES'], in0=B['E'], in1=mS, op=ALU.mult), r=[K('E'), 'cst'], w=[K('ES')])
            P.dve(lambda e: e.tensor_tensor(out=B['EI'], in0=B['E'], in1=mI, op=ALU.mult), r=[K('E'), 'cst'], w=[K('EI')])
            yield
            N0 = B['N'][0]
            P.dve(lambda e: e.scalar_tensor_tensor(out=N0[:, 0, :], in0=pb0[:, 0:128], scalar=nbt[:, c, j16:j16 + 1], in1=B['ES'], op0=ALU.mult, op1=ALU.mult),
                  r=[f'ps{b0}', 'nbt', K('ES')], w=[K('N0a')])
            P.dve(lambda e: e.tensor_tensor(out=B['intra'], in0=pb0[:, 128:256], in1=B['EI'], op=ALU.mult), r=[f'ps{b0}', K('EI')], w=[K('intra')])
            P.dve(lambda e: e.tensor_tensor(out=B['qg'], in0=qc, in1=B['egrow'], op=ALU.mult), r=[qk[0], K('egrow')], w=[K('qg')])
            X32, Xb = B['X32'], B['Xb']
            P.act(lambda e: e.activation(out=B['kdec'], in_=pb1[:, 768:896], func=AF.Copy, scale=kds[:, c, j16:j16 + 1]), r=[f'ps{b1}', 'kds'], w=[K('kdec')])
            P.act(lambda e: e.activation(out=X32[:, 128:256], in_=pb1[:, 768:896], func=AF.Copy, scale=bw[:, c, j16:j16 + 1]), r=[f'ps{b1}', 'bw'], w=[K('X32')])
            P.act(lambda e: e.activation(out=X32[:, 0:128], in_=pb1[:, 896:1024], func=AF.Copy, scale=bt[:, c, j16:j16 + 1]), r=[f'ps{b1}', 'bt'], w=[K('X32')])
            P.act(lambda e: e.copy(Xb, X32), r=[K('X32')], w=[K('Xb')])
            yield
            P.pe(lambda e: e.transpose(pb1[:, 256:384], N0[:, 0, :], ident_b), r=[K('N0a'), 'ident_b'], w=[f'ps{b1}'])
            P.pe(lambda e: e.transpose(pb1[:, 384:512], B['intra'], ident_b), r=[K('intra'), 'ident_b'], w=[f'ps{b1}'])
            P.act(lambda e: e.copy(N0[:, 1, :], pb1[:, 256:384]), r=[f'ps{b1}'], w=[K('N0b')])
            P.act(lambda e: e.copy(B['intraT'], pb1[:, 384:512]), r=[f'ps{b1}'], w=[K('intraT')])
            yield
            bd16 = cst[:, 5, :]
            N0 = B['N'][0]
            Nb = B['N'][1]
            P.dve(lambda e: e.tensor_tensor(out=Nb[:, 0, :], in0=N0[:, 0, :], in1=bd16, op=ALU.mult), r=[K('N0a'), 'cst'], w=[K('N1a')])
            P.dve(lambda e: e.tensor_tensor(out=Nb[:, 1, :], in0=N0[:, 1, :], in1=bd16, op=ALU.mult), r=[K('N0b'), 'cst'], w=[K('N1b')])
            Pc = B['Pm'][0]
            P.dve(lambda e: e.tensor_tensor(out=Pc[:, 0, :], in0=Nb[:, 0, :], in1=ident_f, op=ALU.add), r=[K('N1a'), 'cst'], w=[K('P0')])
            P.dve(lambda e: e.tensor_tensor(out=Pc[:, 1, :], in0=Nb[:, 1, :], in1=ident_f, op=ALU.add), r=[K('N1b'), 'cst'], w=[K('P0')])
            yield
            pi = 0
            Ncur, Ncur_k = Nb, [K('N1a'), K('N1b')]
            scr = [B['M1'], B['Cn']]
            for lev in range(3):
                Nn = scr[lev % 2]
                nk = [K(f'scr{lev % 2}')]
                P.pe(lambda e, Ncur=Ncur: e.matmul(pb2[:, 0:128], Ncur[:, 1, :], Ncur[:, 0, :], start=True, stop=True), r=Ncur_k, w=[f'ps{b2}'])
                P.pe(lambda e, Ncur=Ncur: e.matmul(pb2[:, 128:256], Ncur[:, 0, :], Ncur[:, 1, :], start=True, stop=True), r=Ncur_k, w=[f'ps{b2}'])
                yield
                P.dve(lambda e, Nn=Nn: e.tensor_copy(Nn.rearrange("p a b -> p (a b)"), pb2[:, 0:256]), r=[f'ps{b2}'], w=nk)
                yield
                Pc = B['Pm'][pi]
                Pn = B['Pm'][1 - pi]
                pk, pnk = [K(f'P{pi}')], [K(f'P{1 - pi}')]
                P.pe(lambda e, Pc=Pc: e.matmul(pb2[:, 256:384], ident_b, Pc[:, 0, :], start=True, stop=False), r=pk + ['ident_b'], w=[f'ps{b2}'])
                P.pe(lambda e, Pc=Pc, Nn=Nn: e.matmul(pb2[:, 256:384], Pc[:, 1, :], Nn[:, 0, :], start=False, stop=True), r=pk + nk, w=[f'ps{b2}'])
                P.pe(lambda e, Pc=Pc: e.matmul(pb2[:, 384:512], ident_b, Pc[:, 1, :], start=True, stop=False), r=pk + ['ident_b'], w=[f'ps{b2}'])
                P.pe(lambda e, Pc=Pc, Nn=Nn: e.matmul(pb2[:, 384:512], Nn[:, 0, :], Pc[:, 1, :], start=False, stop=True), r=pk + nk, w=[f'ps{b2}'])
                yield
                P.act(lambda e, Pn=Pn: e.copy(Pn.rearrange("p a b -> p (a b)"), pb2[:, 256:512]), r=[f'ps{b2}'], w=pnk)
                yield
                pi = 1 - pi
                Ncur, Ncur_k = Nn, nk
            for mi in range(3):
                cm = cst[:, 6 + mi, :]
                Dc = B['Pm'][pi]
                Dn = B['Pm'][1 - pi]
                dk, dnk = [K(f'P{pi}')], [K(f'P{1 - pi}')]
                Cn, M1 = B['Cn'], B['M1']
                P.dve(lambda e, cm=cm: e.tensor_tensor(out=Cn[:, 0, :], in0=N0[:, 0, :], in1=cm, op=ALU.mult), r=[K('N0a'), 'cst'], w=[K('scr1')])
                P.dve(lambda e, cm=cm: e.tensor_tensor(out=Cn[:, 1, :], in0=N0[:, 1, :], in1=cm, op=ALU.mult), r=[K('N0b'), 'cst'], w=[K('scr1')])
                yield
                P.pe(lambda e, Dc=Dc: e.matmul(pb2[:, 0:128], Cn[:, 1, :], Dc[:, 0, :], start=True, stop=True), r=[K('scr1')] + dk, w=[f'ps{b2}'])
                P.pe(lambda e, Dc=Dc: e.matmul(pb2[:, 128:256], Cn[:, 0, :], Dc[:, 1, :], start=True, stop=True), r=[K('scr1')] + dk, w=[f'ps{b2}'])
                yield
                P.act(lambda e: e.copy(M1.rearrange("p a b -> p (a b)"), pb2[:, 0:256]), r=[f'ps{b2}'], w=[K('scr0')])
                yield
                P.pe(lambda e, Dc=Dc: e.matmul(pb2[:, 256:384], ident_b, Dc[:, 0, :], start=True, stop=False), r=dk + ['ident_b'], w=[f'ps{b2}'])
                P.pe(lambda e, Dc=Dc: e.matmul(pb2[:, 256:384], Dc[:, 1, :], M1[:, 0, :], start=False, stop=True), r=dk + [K('scr0')], w=[f'ps{b2}'])
                P.pe(lambda e, Dc=Dc: e.matmul(pb2[:, 384:512], ident_b, Dc[:, 1, :], start=True, stop=False), r=dk + ['ident_b'], w=[f'ps{b2}'])
                P.pe(lambda e, Dc=Dc: e.matmul(pb2[:, 384:512], Dc[:, 0, :], M1[:, 1, :], start=False, stop=True), r=dk + [K('scr0')], w=[f'ps{b2}'])
                yield
                P.act(lambda e, Dn=Dn: e.copy(Dn.rearrange("p a b -> p (a b)"), pb2[:, 256:512]), r=[f'ps{b2}'], w=dnk)
                yield
                pi = 1 - pi
            Df = B['Pm'][pi]
            P.pe(lambda e, Df=Df: e.matmul(pb2[:, 256:512], Df[:, 1, :], Xb, start=True, stop=True), r=[K(f'P{pi}'), K('Xb')], w=[f'ps{b2}'])
            yield
            P.dve(lambda e: e.tensor_copy(X32, pb2[:, 256:512]), r=[f'ps{b2}'], w=[K('X32')])
            P.act(lambda e: e.copy(Xb, X32), r=[K('X32')], w=[K('Xb')])
            P.dve(lambda e: e.tensor_tensor(out=B['wlo'], in0=X32[:, 128:256], in1=Xb[:, 128:256], op=ALU.subtract), r=[K('X32'), K('Xb')], w=[K('wlo')])
            yield
            P.pe(lambda e: e.transpose(pb1[:, 0:128], Xb[:, 128:256], ident_b), r=[K('Xb'), 'ident_b'], w=[f'ps{b1}'])
            P.pe(lambda e: e.transpose(pb1[:, 128:256], B['wlo'], ident_b), r=[K('wlo'), 'ident_b'], w=[f'ps{b1}'])
            yield
            P.act(lambda e: e.copy(B['wT'].rearrange("p a b -> p (a b)"), pb1[:, 0:256]), r=[f'ps{b1}'], w=[K('wT')])
            yield
            Sb = Sbf[:, j16, 0, :]
            Sl = Sbf[:, j16, 1, :]
            S3 = S32[:, j16, :]
            sk, s3k = ('Sbf', j16), ('S32', j16)
            P.pe(lambda e: e.matmul(pb3[:, 0:128], B['wT'][:, 0, :], Sb, start=True, stop=False), r=[K('wT'), sk], w=[f'ps{b3}'])
            P.pe(lambda e: e.matmul(pb3[:, 0:128], B['wT'][:, 1, :], Sb, start=False, stop=False), r=[K('wT'), sk], w=[f'ps{b3}'])
            P.pe(lambda e: e.matmul(pb3[:, 0:128], B['wT'][:, 0, :], Sl, start=False, stop=True), r=[K('wT'), sk], w=[f'ps{b3}'])
            yield
            P.dve(lambda e: e.tensor_tensor(out=B['vnew'], in0=X32[:, 0:128], in1=pb3[:, 0:128], op=ALU.subtract), r=[K('X32'), f'ps{b3}'], w=[K('vnew')])
            yield
            P.pe(lambda e: e.matmul(pb3[:, 128:256], B['qg'], Sb, start=True, stop=False), r=[K('qg'), sk], w=[f'ps{b3}'])
            P.pe(lambda e: e.matmul(pb3[:, 128:256], B['intraT'], B['vnew'], start=False, stop=True), r=[K('intraT'), K('vnew')], w=[f'ps{b3}'])
            P.pe(lambda e: e.matmul(pb3[:, 256:384], B['kdec'], B['vnew'], start=True, stop=True), r=[K('kdec'), K('vnew')], w=[f'ps{b3}'])
            yield
            P.act(lambda e: e.copy(B['osb'], pb3[:, 128:256]), r=[f'ps{b3}'], w=[K('osb')])
            P.dma(lambda e: e.dma_start(out=of_d[dr, h, cs, :], in_=B['osb']), r=[K('osb')], w=[('of_d', dr, h, c)], q='pool')
            P.dve(lambda e: e.scalar_tensor_tensor(out=S3, in0=S3, scalar=egt[:, c, j16:j16 + 1], in1=pb3[:, 256:384], op0=ALU.mult, op1=ALU.add),
                  r=[s3k, 'egt', f'ps{b3}'], w=[s3k])
            if (dr == 0 and c == NT // 2 - 1) or (dr == 1 and c == NT // 2):
                P.dve(lambda e: e.tensor_scalar(S3, S3, keepc, None, ALU.mult), r=[s3k, 'keepc'], w=[s3k])
            P.act(lambda e: e.copy(Sb, S3), r=[s3k], w=[sk])
            P.dve(lambda e: e.tensor_tensor(out=Sl, in0=S3, in1=Sb, op=ALU.subtract), r=[s3k, sk], w=[sk])

        todo = []
        for i in range(NT):
            for h in range(8):
                todo.append((i, h, 0))
                todo.append((NT - 1 - i, h, 1))
        todo.reverse()
        active = [None] * NS
        while todo or any(a is not None for a in active):
            for s_ in range(NS):
                if active[s_] is None and todo:
                    active[s_] = chain_step(*todo.pop(), s_)
                if active[s_] is not None:
                    try:
                        next(active[s_])
                    except StopIteration:
                        active[s_] = None
        P.barrier()
        AR.off = PERSIST
        hnb = AR.alloc(128, F32)
        P.dma(lambda e: e.dma_start(out=hnb, in_=hn_in[l:l + 1, :].partition_broadcast(128)), w=['hnb'])
        o1 = [AR.alloc([8, 128], F32) for _ in range(2)]
        o2 = [AR.alloc([8, 128], F32) for _ in range(2)]
        zz = [AR.alloc([8, 128], F32) for _ in range(2)]
        sqj = AR.alloc([8, 128], F32)
        ss = [AR.alloc(8, F32) for _ in range(2)]
        yb = [AR.alloc([8, 128], BF16) for _ in range(2)]
        ytb = [AR.alloc([8, 128], BF16) for _ in range(2)]
        for t in range(NT):
            b = t % 2
            ts_ = slice(t * 128, (t + 1) * 128)
            P.dma(lambda e, b=b, ts_=ts_: e.dma_start(out=o1[b], in_=of_d[0][:, ts_, :].rearrange("h t d -> t h d")), w=[f'o1{b}'])
            P.dma(lambda e, b=b, ts_=ts_: e.dma_start(out=o2[b], in_=of_d[1][:, ts_, :].rearrange("h t d -> t h d")), w=[f'o2{b}'])
            P.dma(lambda e, b=b, ts_=ts_: e.dma_start(out=zz[b].rearrange("p a b -> p (a b)"), in_=z_d[ts_, :]), w=[f'zz{b}'])
            P.dve(lambda e, b=b: e.tensor_tensor(out=o1[b], in0=o1[b], in1=o2[b], op=ALU.add), r=[f'o1{b}', f'o2{b}'], w=[f'o1{b}'])
            P.dve(lambda e, b=b: e.tensor_tensor(out=sqj, in0=o1[b], in1=o1[b], op=ALU.mult), r=[f'o1{b}'], w=['sqj'])
            P.dve(lambda e, b=b: e.tensor_reduce(out=ss[b], in_=sqj, axis=AX.X, op=ALU.add), r=['sqj'], w=[f'ss{b}'])
            P.act(lambda e, b=b: e.activation(out=ss[b], in_=ss[b], func=AF.Sqrt, scale=1.0 / 128, bias=epsc), r=[f'ss{b}', 'epsc'], w=[f'ss{b}'])
            P.dve(lambda e, b=b: e.reciprocal(ss[b], ss[b]), r=[f'ss{b}'], w=[f'ss{b}'])
            P.dve(lambda e, b=b: e.tensor_tensor(out=o1[b], in0=o1[b], in1=ss[b].unsqueeze(2).to_broadcast([128, 8, 128]), op=ALU.mult),
                  r=[f'o1{b}', f'ss{b}'], w=[f'o1{b}'])
            P.dve(lambda e, b=b: e.tensor_tensor(out=o1[b], in0=o1[b], in1=hnb.unsqueeze(1).to_broadcast([128, 8, 128]), op=ALU.mult),
                  r=[f'o1{b}', 'hnb'], w=[f'o1{b}'])
            P.dve(lambda e, b=b: e.tensor_tensor(out=yb[b], in0=o1[b], in1=zz[b], op=ALU.mult), r=[f'o1{b}', f'zz{b}'], w=[f'yb{b}'])
            pb = bank(6 + b, BF16).rearrange("p (a b) -> p a b", b=128)
            for j in range(8):
                P.pe(lambda e, b=b, j=j, pb=pb: e.transpose(pb[:, j, :], yb[b][:, j, :], ident_b), r=[f'yb{b}', 'ident_b'], w=[f'ps{6 + b}'])
            P.act(lambda e, b=b, pb=pb: e.copy(ytb[b], pb), r=[f'ps{6 + b}'], w=[f'ytb{b}'])
            P.dma(lambda e, t=t, b=b: e.dma_start(out=yaT_d[:, :, t * 128:(t + 1) * 128].rearrange("c p t -> p c t"), in_=ytb[b]),
                  r=[f'ytb{b}'], w=[('yaT_d', t)], q='pool')
        P.barrier()
    def run_all():
        for l in range(L):
            src = x_in if l == 0 else h_d
            phase_norm(src, 2 * l)
            if cfg.stop == 'norm':
                return
            phase_proj(l)
            if cfg.stop == 'proj':
                return
            phase_attn()
            if cfg.stop == 'attn':
                return
            phase_gdn(l)
            if cfg.stop == 'gdn':
                return
            phase_merge(l)
            phase_wout(l, src)
            if cfg.stop == 'wout':
                return
            phase_norm(h_d, 2 * l + 1)
            phase_mlp(l)
            if cfg.stop == 'mlp':
                return
        phase_norm(h_d, 2 * DEPTH, final_out=y_out)

    run_all()
    P.barrier()
    P.emit(es)
    es.close()
    return nc, P, outs


def _t5_bucket(rel):
    half = 16
    exact = 8
    ret = np.where(rel > 0, half, 0)
    n = np.abs(rel)
    large = exact + (np.log(np.maximum(n, 1) / exact) / np.log(2048 / exact) * (half - exact)).astype(np.int32)
    large = np.minimum(large, half - 1)
    return (ret + np.where(n < exact, n, large)).astype(np.int32)


def host_consts():
    i = np.arange(128)
    eye = np.eye(128, dtype=np.float32)
    tril_i = (i[:, None] >= i[None, :]).astype(np.float32)
    tril_s = (i[:, None] > i[None, :]).astype(np.float32)
    triu_i = (i[:, None] <= i[None, :]).astype(np.float32)
    triu_s = (i[:, None] < i[None, :]).astype(np.float32)
    blk = lambda b: (i[:, None] // b == i[None, :] // b).astype(np.float32)
    bd16 = blk(16)
    cms = [blk(2 * b) - blk(b) for b in (16, 32, 64)]
    consts = np.stack([eye, tril_i, tril_s, triu_i, triu_s, bd16] + cms, axis=1).astype(np.float32)
    jmat = eye[::-1].copy()
    eoh = np.zeros((32, 3, 768), np.float32)
    amask = np.zeros((128, 3, 128), np.float32)
    for dl in range(3):
        rel_kq = 128 * (dl - 1) + i[:, None] - i[None, :]
        amask[:, dl, :] = np.where(np.abs(rel_kq) <= 64, 0.0, -30000.0)
        for g, dil in enumerate((1, 4, 16)):
            ii = np.arange(255)
            rel = 128 * (dl - 1) + ii - 127
            ok = np.abs(rel) <= 64
            b = _t5_bucket(rel * dil)
            eoh[b[ok], g, dl * 256 + ii[ok]] = 1.0
    return consts, jmat, eoh, amask


def make_in_map(x, keep, p):
    consts, jmat, eoh, amask = host_consts()
    m = {
        'x': np.ascontiguousarray(x, dtype=np.float32),
        'keep': np.full((128, 1), keep, np.float32),
        'consts': consts, 'jmat': jmat, 'eoh': eoh, 'amask': amask,
        'w_in0': np.ascontiguousarray(p['w_in'][0]), 'w_in1': np.ascontiguousarray(p['w_in'][1]),
        'w_br_a': p['w_br_a'], 'w_br_b': p['w_br_b'], 'w_br_c': p['w_br_c'], 'w_out': p['w_out'],
        'w_up': p['w_up'], 'w_down': p['w_down'],
        'norms': np.stack([p['norm_mix'][0], p['norm_mlp'][0], p['norm_mix'][1], p['norm_mlp'][1], p['norm_final']]).astype(np.float32),
        'conva': np.ascontiguousarray(p['conv_a'].reshape(DEPTH, 3, 24, 128).transpose(0, 3, 2, 1).reshape(DEPTH, 128, 72)),
        'convb': np.ascontiguousarray(p['conv_b'].reshape(DEPTH, 3, 8, 128).transpose(0, 3, 2, 1).reshape(DEPTH, 128, 24)),
        'gdnp': np.concatenate([p['a_log'].reshape(DEPTH, 16), p['dt_bias'].reshape(DEPTH, 16)], axis=1).astype(np.float32),
        'hn': np.ascontiguousarray(p['head_norm']),
        'relb': np.ascontiguousarray(p['rel_bias']),
    }
    return m


_CACHE = {}


def kernel(**inputs):
    p = {k: np.asarray(v, dtype=np.float32) for k, v in inputs.items()}
    xp, xs = p['x_prompt'], p['x_sample']
    T = 4096
    if 'nc' not in _CACHE:
        _CACHE['nc'] = build(Cfg(T=T))
    nc = _CACHE['nc'][0]
    jobs = {0: (xp[0], 1.0), 1: (xp[1], 1.0), 4: (xs[0:2].reshape(T, D), 0.0), 5: (xs[2:4].reshape(T, D), 0.0)}
    maps = [None] * 8
    for c, (xc, kp) in jobs.items():
        maps[c] = make_in_map(xc, kp, p)
    zmap = {k: (v if k in ('consts', 'jmat', 'eoh', 'amask') else np.zeros_like(v)) for k, v in maps[0].items()}
    for c in range(8):
        if maps[c] is None:
            maps[c] = zmap
    res = run_bass_kernel_spmd(nc, maps, core_ids=list(range(8)))
    ys = [np.asarray(res.results[c]['y'], dtype=np.float32) for c in (0, 1, 4, 5)]
    y_prompt = np.stack([ys[0], ys[1]], axis=0)
    y_sample = np.concatenate([ys[2].reshape(2, 2048, D), ys[3].reshape(2, 2048, D)], axis=0)
    return (y_prompt, y_sample)
```
